# Optimizing a Trainium2 kernel written in Bass

```python
import jax, jax.numpy as jnp
from jax import lax
import numpy as np

D_MODEL = 1024
BATCH = 8
SEQ = 2048
DEPTH = 4
DEC_BATCH = 128
DEC_SEQ = 1
PAST_LEN = 16384
PAGE_SIZE = 128

W_A = D_MODEL // 4
W_B = D_MODEL // 4
W_C = D_MODEL // 2
MIX_WIDTH = W_A + W_B + W_C
K_A = 3
K_B = 31
K_C = 4
LRU_HEADS = 8
LRU_HD = W_C // LRU_HEADS
LRU_C = 8.0
D_FF = 2816
RMS_EPS = 1e-6
LN_EPS = 1e-5
IN_SIZES = (W_A, W_A, W_A, W_B, W_B, W_C, W_C)
IN_WIDTH = sum(IN_SIZES)
IN_SPLITS = [int(s) for s in np.cumsum(IN_SIZES)[:-1]]

kernel_name = "hybrid_conv_lru_macaron_decode_step"


def rms_norm(x, g):
    xf = x.astype(jnp.float32)
    y = xf * lax.rsqrt(jnp.mean(xf * xf, axis=-1, keepdims=True) + RMS_EPS)
    return (y * g.astype(jnp.float32)).astype(x.dtype)


def layer_norm(x, g, b):
    xf = x.astype(jnp.float32)
    mu = jnp.mean(xf, axis=-1, keepdims=True)
    xc = xf - mu
    var = jnp.mean(xc * xc, axis=-1, keepdims=True)
    y = xc * lax.rsqrt(var + LN_EPS) * g.astype(jnp.float32) + b.astype(jnp.float32)
    return y.astype(x.dtype)


def swiglu(h, w_up, w_down):
    gu = h @ w_up
    g, u = jnp.split(gu, 2, axis=-1)
    return (jax.nn.silu(g) * u) @ w_down


def causal_dwconv(x, buf, w):
    K, C = w.shape
    xp = jnp.concatenate([buf.astype(x.dtype), x], axis=1)
    y = lax.conv_general_dilated(xp, w[:, None, :].astype(x.dtype), window_strides=(1,), padding='VALID',
                                 dimension_numbers=('NWC', 'WIO', 'NWC'), feature_group_count=C)
    return y, xp[:, xp.shape[1] - (K - 1):]


def lru_scan(a, b, h0):
    def comb(l, r):
        return (l[0] * r[0], r[0] * l[1] + r[1])
    A, Bc = lax.associative_scan(comb, (a, b), axis=1)
    return A * h0[:, None, :] + Bc


def block_diag(x, w):
    Bsz, T, C = x.shape
    y = jnp.einsum('bthi,hij->bthj', x.reshape(Bsz, T, LRU_HEADS, LRU_HD), w)
    return y.reshape(Bsz, T, C)


def mixer(h, buf_a, buf_b, buf_c, h0, w_in, conv_a_w, conv_b_w, conv_b_b, ln_b_g, ln_b_b,
          conv_c_w, conv_c_b, lru_wa, lru_ba, lru_wx, lru_bx, lru_lam, grp_g, w_out):
    u = h @ w_in
    a_b, a_c, a_x, b_v, b_g, c_g, c_x = jnp.split(u, IN_SPLITS, axis=-1)
    conv_a, nbuf_a = causal_dwconv(a_c * a_x, buf_a, conv_a_w)
    y_a = a_b * conv_a
    glu = b_v * jax.nn.sigmoid(b_g)
    conv_b, nbuf_b = causal_dwconv(glu, buf_b, conv_b_w)
    y_b = jax.nn.silu(layer_norm(conv_b + conv_b_b, ln_b_g, ln_b_b))
    xc, nbuf_c = causal_dwconv(c_x, buf_c, conv_c_w)
    xc = xc + conv_c_b
    r = jax.nn.sigmoid((block_diag(xc, lru_wa) + lru_ba).astype(jnp.float32))
    i = jax.nn.sigmoid((block_diag(xc, lru_wx) + lru_bx).astype(jnp.float32))
    log_a = -LRU_C * r * jax.nn.softplus(-lru_lam.astype(jnp.float32))
    a = jnp.exp(log_a)
    b = jnp.sqrt(jnp.maximum(-jnp.expm1(2.0 * log_a), 0.0)) * (i * xc.astype(jnp.float32))
    hs = lru_scan(a, b, h0.astype(jnp.float32))
    y_c = jax.nn.gelu(c_g) * hs.astype(h.dtype)
    y = jnp.concatenate([rms_norm(y_a, grp_g[:W_A]),
                         rms_norm(y_b, grp_g[W_A:W_A + W_B]),
                         rms_norm(y_c, grp_g[W_A + W_B:])], axis=-1)
    return y @ w_out, nbuf_a, nbuf_b, nbuf_c, hs[:, -1].astype(h0.dtype)


def run_trunk(x, st_a, st_b, st_c, st_h, norm_ffn1, w1_up, w1_down, norm_mix, w_in, conv_a_w, conv_b_w,
              conv_b_b, ln_b_g, ln_b_b, conv_c_w, conv_c_b, lru_wa, lru_ba, lru_wx, lru_bx, lru_lam, grp_g,
              w_out, norm_ffn2, w2_up, w2_down, final_norm):
    na, nb, nc, nh = [], [], [], []
    for l in range(DEPTH):
        x = x + 0.5 * swiglu(rms_norm(x, norm_ffn1[l]), w1_up[l], w1_down[l])
        m, ba, bb, bc, hl = mixer(rms_norm(x, norm_mix[l]), st_a[l], st_b[l], st_c[l], st_h[l], w_in[l],
                                  conv_a_w[l], conv_b_w[l], conv_b_b[l], ln_b_g[l], ln_b_b[l], conv_c_w[l],
                                  conv_c_b[l], lru_wa[l], lru_ba[l], lru_wx[l], lru_bx[l], lru_lam[l],
                                  grp_g[l], w_out[l])
        x = x + m
        x = x + 0.5 * swiglu(rms_norm(x, norm_ffn2[l]), w2_up[l], w2_down[l])
        na.append(ba); nb.append(bb); nc.append(bc); nh.append(hl)
    return (rms_norm(x, final_norm), jnp.stack(na), jnp.stack(nb), jnp.stack(nc), jnp.stack(nh))


def setup_inputs(seed: int = 0) -> dict:
    key = jax.random.key(seed)
    ks = jax.random.split(key, 40)
    f32 = jnp.float32

    def nrm(k, shape, scale):
        return jax.random.normal(k, shape, f32) * scale

    def gain(k, shape):
        return 1.0 + 0.01 * jax.random.normal(k, shape, f32)

    u = jax.random.uniform(ks[39], (DEPTH, W_C), f32, minval=0.9, maxval=0.999)
    a0 = u ** (1.0 / LRU_C)
    lam = jnp.log(a0) - jnp.log1p(-a0)
    return {
        "x_prompt": nrm(ks[0], (BATCH, SEQ, D_MODEL), 1.0),
        "x_sample": nrm(ks[1], (DEC_BATCH, DEC_SEQ, D_MODEL), 1.0),
        "state_conv_a": nrm(ks[2], (DEPTH, DEC_BATCH, K_A - 1, W_A), 1.0),
        "state_conv_b": nrm(ks[3], (DEPTH, DEC_BATCH, K_B - 1, W_B), 1.0),
        "state_conv_c": nrm(ks[4], (DEPTH, DEC_BATCH, K_C - 1, W_C), 1.0),
        "state_lru_h": nrm(ks[5], (DEPTH, DEC_BATCH, W_C), 0.5),
        "norm_ffn1": gain(ks[6], (DEPTH, D_MODEL)),
        "w1_up": nrm(ks[7], (DEPTH, D_MODEL, 2 * D_FF), D_MODEL ** -0.5),
        "w1_down": nrm(ks[8], (DEPTH, D_FF, D_MODEL), D_FF ** -0.5),
        "norm_mix": gain(ks[9], (DEPTH, D_MODEL)),
        "w_in": nrm(ks[10], (DEPTH, D_MODEL, IN_WIDTH), D_MODEL ** -0.5),
        "conv_a_w": nrm(ks[11], (DEPTH, K_A, W_A), K_A ** -0.5),
        "conv_b_w": nrm(ks[12], (DEPTH, K_B, W_B), K_B ** -0.5),
        "conv_b_b": nrm(ks[13], (DEPTH, W_B), 0.01),
        "ln_b_g": gain(ks[14], (DEPTH, W_B)),
        "ln_b_b": nrm(ks[15], (DEPTH, W_B), 0.01),
        "conv_c_w": nrm(ks[16], (DEPTH, K_C, W_C), K_C ** -0.5),
        "conv_c_b": nrm(ks[17], (DEPTH, W_C), 0.01),
        "lru_wa": nrm(ks[18], (DEPTH, LRU_HEADS, LRU_HD, LRU_HD), LRU_HD ** -0.5),
        "lru_ba": nrm(ks[19], (DEPTH, W_C), 0.01),
        "lru_wx": nrm(ks[20], (DEPTH, LRU_HEADS, LRU_HD, LRU_HD), LRU_HD ** -0.5),
        "lru_bx": nrm(ks[21], (DEPTH, W_C), 0.01),
        "lru_lam": lam,
        "grp_g": gain(ks[22], (DEPTH, MIX_WIDTH)),
        "w_out": nrm(ks[23], (DEPTH, MIX_WIDTH, D_MODEL), MIX_WIDTH ** -0.5),
        "norm_ffn2": gain(ks[24], (DEPTH, D_MODEL)),
        "w2_up": nrm(ks[25], (DEPTH, D_MODEL, 2 * D_FF), D_MODEL ** -0.5),
        "w2_down": nrm(ks[26], (DEPTH, D_FF, D_MODEL), D_FF ** -0.5),
        "final_norm": gain(ks[27], (D_MODEL,)),
    }


def reference(x_prompt, x_sample, state_conv_a, state_conv_b, state_conv_c, state_lru_h,
              norm_ffn1, w1_up, w1_down, norm_mix, w_in, conv_a_w, conv_b_w, conv_b_b, ln_b_g, ln_b_b,
              conv_c_w, conv_c_b, lru_wa, lru_ba, lru_wx, lru_bx, lru_lam, grp_g, w_out,
              norm_ffn2, w2_up, w2_down, final_norm):
    weights = (norm_ffn1, w1_up, w1_down, norm_mix, w_in, conv_a_w, conv_b_w, conv_b_b, ln_b_g, ln_b_b,
               conv_c_w, conv_c_b, lru_wa, lru_ba, lru_wx, lru_bx, lru_lam, grp_g, w_out,
               norm_ffn2, w2_up, w2_down, final_norm)
    bp = x_prompt.shape[0]
    zp_a = jnp.zeros((DEPTH, bp, K_A - 1, W_A), x_prompt.dtype)
    zp_b = jnp.zeros((DEPTH, bp, K_B - 1, W_B), x_prompt.dtype)
    zp_c = jnp.zeros((DEPTH, bp, K_C - 1, W_C), x_prompt.dtype)
    zp_h = jnp.zeros((DEPTH, bp, W_C), x_prompt.dtype)
    y_prompt, p_conv_a, p_conv_b, p_conv_c, p_lru_h = run_trunk(x_prompt, zp_a, zp_b, zp_c, zp_h, *weights)
    y_sample, s_conv_a, s_conv_b, s_conv_c, s_lru_h = run_trunk(x_sample, state_conv_a, state_conv_b,
                                                               state_conv_c, state_lru_h, *weights)
    return (y_prompt, y_sample, p_conv_a, p_conv_b, p_conv_c, p_lru_h, s_conv_a, s_conv_b, s_conv_c, s_lru_h)
```

```python
import numpy as np
from contextlib import ExitStack
import concourse.bass as bass
import concourse.mybir as mybir
from concourse.bass_utils import run_bass_kernel_spmd

F32 = mybir.dt.float32
BF16 = mybir.dt.bfloat16
AF = mybir.ActivationFunctionType
ALU = mybir.AluOpType
AX = mybir.AxisListType

NCORES = 8
D = 1024
T = 2048
NS = 16
NT = T + NS
DEPTH = 4
DFF = 2816
GR = 2048
NGR = 6
FFN_TILES = [(0, 512), (512, 512), (1024, 512), (1536, 256), (1792, 272)]
CH = 1024
MIX_TILES = [(256 * i, 256) for i in range(8)] + [(2048, 16)]
NPIECE = 11
RMS_EPS = 1e-6
LN_EPS = 1e-5

PC_NF1, PC_NMIX, PC_NF2, PC_GG = 0, 8, 16, 24
PC_CAW = 32
PC_CBW = 38
PC_CBB = 100
PC_LNG = 102
PC_LNB = 104
PC_CCW = 106
PC_CCB = 122
PC_BA = 126
PC_BX = 130
PC_LAM = 134
PL = 138
NPRM = PL * DEPTH + 8

UNIT_F = 6144


def layer_units():
    units = []
    for i in range(NPIECE):
        units.append(("F1", i, UNIT_F, 3 * ((i + 1) % 2)))
    units.append(("MA1", 0, 6144, 0))
    units.append(("MA2", 0, 2048, 5))
    units.append(("MB1", 0, 4096, 3))
    units.append(("MB2", 0, 2048, 2))
    units.append(("MC1", 0, 4096, 0))
    units.append(("MC2", 0, 4096, 4))
    units.append(("MC3", 0, 4096, 2))
    for i in range(NPIECE):
        units.append(("F2", i, UNIT_F, 3 * (i % 2)))
    return units


LUNITS = layer_units()
LSTREAM = sum(u[2] for u in LUNITS)


class Eng:
    def __init__(self, name, h, sem):
        self.name, self.h, self.sem, self.count, self.waited = name, h, sem, 0, {}


class Sched:
    def __init__(self, nc, es, ndma=24):
        self.nc = nc
        self.eng = {}
        for n, h in [("pe", nc.tensor), ("act", nc.scalar), ("dve", nc.vector),
                     ("pool", nc.gpsimd), ("sp", nc.sync)]:
            self.eng[n] = Eng(n, h, es.enter_context(nc.semaphore("sem_" + n)))
        self.dsem = [es.enter_context(nc.semaphore(f"dsem{i}")) for i in range(ndma)]
        self.dcount = [0] * ndma
        self.dlast = [None] * ndma
        self.dnext = 0
        self.lastw = {}
        self.readers = {}
        self.out_tickets = []

    def _wait(self, e, tk):
        if tk is None:
            return
        sem, val, name = tk
        if e.waited.get(name, 0) >= val:
            return
        e.h.wait_ge(sem, val)
        e.waited[name] = val

    def _deps(self, e, reads, writes):
        for k in reads:
            self._wait(e, self.lastw.get(k))
        for k in writes:
            self._wait(e, self.lastw.get(k))
            for tk in self.readers.get(k, {}).values():
                self._wait(e, tk)

    def _commit(self, tk, reads, writes):
        for k in reads:
            self.readers.setdefault(k, {})[tk[2]] = tk
        for k in writes:
            self.lastw[k] = tk
            self.readers[k] = {}

    def op(self, en, fn, reads=(), writes=()):
        e = self.eng[en]
        self._deps(e, reads, writes)
        ins = fn(e.h)
        e.count += 1
        ins.then_inc(e.sem, 1)
        tk = (e.sem, e.count, en)
        self._commit(tk, reads, writes)
        return tk

    def group(self, fns, reads=(), writes=()):
        e = self.eng["pe"]
        self._deps(e, reads, writes)
        for f in fns[:-1]:
            f(e.h)
        ins = fns[-1](e.h)
        e.count += 1
        ins.then_inc(e.sem, 1)
        tk = (e.sem, e.count, "pe")
        self._commit(tk, reads, writes)
        return tk

    def dma(self, qn, out, in_, reads=(), writes=(), is_output=False):
        e = self.eng[qn]
        self._deps(e, reads, writes)
        i = self.dnext
        self.dnext = (i + 1) % len(self.dsem)
        self._wait(e, self.dlast[i])
        ins = e.h.dma_start(out=out, in_=in_)
        self.dcount[i] += 16
        ins.then_inc(self.dsem[i], 16)
        tk = (self.dsem[i], self.dcount[i], f"d{i}")
        self.dlast[i] = tk
        self._commit(tk, reads, writes)
        if is_output:
            self.out_tickets.append(tk)
        return tk


def build(depth=DEPTH, do_mixer=True, do_ffn=True):
    nc = bass.Bass("TRN2", target_bir_lowering=False)

    def din(name, shape):
        return nc.dram_tensor(name, list(shape), F32, kind="ExternalInput").ap()

    def dout(name, shape):
        return nc.dram_tensor(name, list(shape), F32, kind="ExternalOutput").ap()

    xp = din("xp", [T, D])
    xs = din("xs", [NS, D])
    sta = din("sta", [DEPTH, NS * 2, 256])
    stb = din("stb", [DEPTH, NS * 30, 256])
    stc = din("stc", [DEPTH, NS * 3, 512])
    sth = din("sth", [DEPTH, NS, 512])
    ws = din("ws", [DEPTH, 128, LSTREAM])
    prm = din("prm", [128, NPRM])
    lwa = din("lwa", [DEPTH, 8, 64, 64])
    lwx = din("lwx", [DEPTH, 8, 64, 64])
    yp = dout("yp", [T, D])
    ys = dout("ys", [NS, D])
    opa = dout("opa", [DEPTH, 2, 256])
    opb = dout("opb", [DEPTH, 30, 256])
    opc = dout("opc", [DEPTH, 3, 512])
    oph = dout("oph", [DEPTH, 1, 512])
    osa = dout("osa", [DEPTH, NS * 2, 256])
    osb = dout("osb", [DEPTH, NS * 30, 256])
    osc = dout("osc", [DEPTH, NS * 3, 512])
    osh = dout("osh", [DEPTH, NS, 512])

    with ExitStack() as es:
        S = Sched(nc, es)
        op, group, dma = S.op, S.group, S.dma

        def sb(name, shape, dt):
            return es.enter_context(nc.sbuf_tensor(name, list(shape), dt))

        X = sb("X", [128, 8, NT], F32)
        HY = sb("HY", [128, 8, NT], BF16)
        RING = sb("RING", [128, NGR * GR], BF16)
        STG = sb("STG", [128, 3, CH], F32)
        ACTB = sb("ACTB", [128, 2, 2, 512], BF16)
        TC = sb("TC", [128, 32, 256], F32)
        SQB = sb("SQB", [128, 8, 272], BF16)
        XCB = sb("XCB", [128, 4, 256], BF16)
        YT = sb("YT", [128, 2, 4, 256], BF16)
        GL = sb("GL", [128, 2, 542], F32)
        XC = sb("XC", [128, 4, 259], F32)
        XPA = sb("XPA", [128, 2, NS, 3], F32)
        XPB = sb("XPB", [128, 2, NS, 31], F32)
        XPC = sb("XPC", [128, 4, NS, 4], F32)
        HS0 = sb("HS0", [128, 4, NS], F32)
        HC = sb("HC", [128, 4], F32)
        IO = sb("IO", [128, 3, 512], F32)
        PRM = sb("PRM", [128, NPRM], F32)
        ONES = sb("ONES", [128, 128], BF16)
        ONEF = sb("ONEF", [128, 128], F32)
        IDENT = sb("IDENT", [128, 128], F32)
        BD = sb("BD", [128, 2, 4, 128], BF16)
        NSP = sb("NSP", [128, 16], F32)
        PS = [es.enter_context(nc.psum_tensor(f"ps{i}", [128, 512], F32)) for i in range(8)]

        def xk(ms, c0, n):
            return [("X", m, b) for m in ms for b in range(c0 // 256, (c0 + n - 1) // 256 + 1)]

        def hk(ks, c0, n):
            return [("H", k, b) for k in ks for b in range(c0 // 256, (c0 + n - 1) // 256 + 1)]

        def tk_(*ids):
            return [("T", i) for i in ids]

        def psh(h, n=256):
            return PS[h][:, 0:n]

        def pk(h):
            return [("P", h)]

        def psb(b, n=512):
            return PS[b][:, 0:n]

        def pkb(b):
            return [("P", b)]

        rot = {"m": 0, "s": 0, "x": 0}

        def mps():
            h = rot["m"]
            rot["m"] = (h + 1) % 6
            return h

        def sps():
            h = 6 + rot["s"]
            rot["s"] = (rot["s"] + 1) % 2
            return h

        xps = sps

        def prmc(col, n=1):
            return PRM[:, col:col + n]

        alt = {"i": 0}

        def copy_any(out, in_, reads, writes):
            alt["i"] ^= 1
            if alt["i"]:
                return op("act", lambda h: h.activation(out=out, in_=in_, func=AF.Copy), reads, writes)
            return op("dve", lambda h: h.tensor_copy(out=out, in_=in_), reads, writes)

        units = []
        for l in range(depth):
            off = 0
            for (kind, i, size, g0) in LUNITS:
                units.append(dict(l=l, kind=kind, i=i, off=off, size=size, g0=g0, done=False))
                off += size
        uidx = {(u["l"], u["kind"], u["i"]): n for n, u in enumerate(units)}
        owner = [None] * NGR
        ld = {"u": 0, "c": 0, "s": 0}

        def pump(nmax=1):
            n = 0
            while n < nmax and ld["u"] < len(units):
                u = units[ld["u"]]
                skip = (u["kind"][0] == "F" and not do_ffn) or (u["kind"][0] == "M" and not do_mixer)
                if skip:
                    ld["u"] += 1
                    ld["c"] = 0
                    continue
                eo = ld["c"] * CH
                g = u["g0"] + eo // GR
                if eo % GR == 0:
                    if owner[g] is not None and not units[owner[g]]["done"]:
                        return
                    owner[g] = ld["u"]
                s = ld["s"]
                ld["s"] = (s + 1) % 3
                o = u["off"] + eo
                ro = u["g0"] * GR + eo
                dma("sp", STG[:, s, :], ws[u["l"], :, o:o + CH], reads=[], writes=[("S", s)])
                op("act", lambda h: h.activation(out=RING[:, ro:ro + CH], in_=STG[:, s, :], func=AF.Copy),
                   reads=[("S", s)], writes=[("R", g)])
                ld["c"] += 1
                if ld["c"] * CH >= u["size"]:
                    ld["u"] += 1
                    ld["c"] = 0
                n += 1

        def unit(l, kind, i=0):
            n = uidx[(l, kind, i)]
            u = units[n]
            guard = 0
            while ld["u"] <= n:
                before = (ld["u"], ld["c"])
                pump(1)
                guard += 1
                assert (ld["u"], ld["c"]) != before, f"loader stuck at unit {units[ld['u']]} wanting {u}"
            base = u["g0"] * GR
            keys = [("R", u["g0"] + j) for j in range(u["size"] // GR)]
            return base, keys, n

        def unit_done(n):
            units[n]["done"] = True

        dma("sp", PRM[:, :], prm[:, :], writes=["PRM"])
        op("pool", lambda h: h.memset(ONES[:], 1.0), writes=["ONES"])
        op("pool", lambda h: h.memset(ONEF[:], 1.0), writes=["ONEF"])
        op("pool", lambda h: h.memset(BD[:], 0.0), writes=["BD"])
        op("pool", lambda h: h.affine_select(out=IDENT[:], in_=ONEF[:], pattern=[[1, 128]],
                                             compare_op=ALU.is_equal, fill=0.0, base=0,
                                             channel_multiplier=-1),
           reads=["ONEF"], writes=["IDENT"])
        op("dve", lambda h: h.memset(HC[:], 0.0), writes=["HC"])

        def load_x_block(src, r0, nrow, c0):
            s = ld["s"]
            ld["s"] = (s + 1) % 3
            dma("sp", STG[0:nrow, s, 0:1024], src[r0:r0 + nrow, :], writes=[("S", s)])
            for half in range(2):
                b = 6 + half
                fns = []
                for cc in range(4):
                    c = half * 4 + cc
                    fns.append(lambda h, c=c, cc=cc, b=b: h.transpose(
                        out=PS[b][:, cc * 128:cc * 128 + nrow], in_=STG[0:nrow, s, c * 128:(c + 1) * 128],
                        identity=IDENT[0:nrow, 0:nrow]))
                group(fns, reads=[("S", s), "IDENT"], writes=pkb(b))
                src_ap = PS[b][:, :].rearrange("p (c t) -> p c t", c=4)[:, :, 0:nrow]
                copy_any(X[:, half * 4:half * 4 + 4, c0:c0 + nrow], src_ap, reads=pkb(b),
                         writes=xk(range(half * 4, half * 4 + 4), c0, nrow))

        PRE = []

        def load_all_x():
            for blk in range(T // 128):
                load_x_block(xp, blk * 128, 128, blk * 128)
                if blk % 2 == 1:
                    PRE[0](blk // 2 * 256, 256)
            load_x_block(xs, 0, NS, T)
            PRE[0](T, NS)

        def norm_parts(c0, n, gcol, out_fn, out_keys_fn):
            rsk = tk_(19) if n <= 256 else tk_(19, 20)
            rs = TC[:, 19:21, :].rearrange("p a b -> p (a b)")[:, 0:n]
            st = {}

            def f_sq():
                op("act", lambda h: h.activation(out=SQB[:, :, 0:n], in_=X[:, :, c0:c0 + n], func=AF.Square),
                   reads=xk(range(8), c0, n), writes=[("SQ", c) for c in range(8)])

            def f_mid():
                hs_ = sps()
                group([lambda h, c=c: h.matmul(psh(hs_, n), lhsT=ONES[:, :], rhs=SQB[:, c, 0:n],
                                               start=(c == 0), stop=(c == 7)) for c in range(8)],
                      reads=[("SQ", c) for c in range(8)] + ["ONES"], writes=pk(hs_))
                op("act", lambda h: h.activation(out=rs, in_=psh(hs_, n), func=AF.Ln, scale=1.0 / D, bias=EPS_RMS[:, 0:1]),
                   reads=pk(hs_) + ["EPS"], writes=rsk)
                op("act", lambda h: h.activation(out=rs, in_=rs, func=AF.Exp, scale=-0.5),
                   reads=rsk, writes=rsk)

            def f_stt():
                for c in range(8):
                    op("dve", lambda h, c=c: h.scalar_tensor_tensor(out=out_fn(c), in0=X[:, c, c0:c0 + n],
                                                                    scalar=prmc(gcol + c), in1=rs,
                                                                    op0=ALU.mult, op1=ALU.mult),
                       reads=xk([c], c0, n) + rsk + ["PRM"], writes=out_keys_fn(c))
            return f_sq, f_mid, f_stt

        def norm_block(c0, n, gcol, out_fn, out_keys_fn, defer=False):
            f_sq, f_mid, f_stt = norm_parts(c0, n, gcol, out_fn, out_keys_fn)
            f_sq()
            f_mid()
            if defer:
                return f_stt
            f_stt()

        EPSB = sb("EPSB", [128, 4], F32)
        EPS_RMS = EPSB[:, 0:1]
        EPS_LN = EPSB[:, 1:2]
        op("dve", lambda h: h.memset(EPSB[:, 0:1], RMS_EPS), writes=["EPS0"])
        op("dve", lambda h: h.memset(EPSB[:, 1:2], LN_EPS), writes=["EPS1"])
        op("dve", lambda h: h.memset(EPSB[:, 2:3], 1.0), writes=["EPS2"])
        S.lastw["EPS"] = S.lastw["EPS1"]

        def norm_to_hy(gcol):
            for (c0, n) in MIX_TILES:
                norm_block(c0, n, gcol, lambda c, c0=c0, n=n: HY[:, c, c0:c0 + n],
                           lambda c, c0=c0, n=n: hk([c], c0, n))

        HB = [2, 6, 16]
        hbi = {"i": 0}
        def ffn(l, kind, gcol, pre_norm, post, mid=None):
            if pre_norm:
                norm_to_hy(gcol)
            seq = [(i, t) for i in range(NPIECE) for t in range(len(FFN_TILES))]
            prev = None
            ucur = {}
            postq = []

            def advance():
                while postq:
                    try:
                        next(postq[0])
                        return
                    except StopIteration:
                        postq.pop(0)

            def emit_D(i, t, ab, base, rkeys, un):
                c0, n = FFN_TILES[t]
                wd = base + 4096
                for m in range(8):
                    b = 4 + (m % 4)
                    group([lambda h, j=j, m=m, b=b: h.matmul(
                        psb(b, n), lhsT=RING[:, wd + j * 1024 + m * 128: wd + j * 1024 + (m + 1) * 128],
                        rhs=ACTB[:, ab, j, 0:n], start=(j == 0), stop=(j == 1)) for j in range(2)],
                        reads=rkeys + [("A", ab, 0), ("A", ab, 1)], writes=pkb(b))
                    if m % 2 == 0:
                        op("dve", lambda h, m=m, b=b: h.scalar_tensor_tensor(
                            out=X[:, m, c0:c0 + n], in0=psb(b, n), scalar=0.5, in1=X[:, m, c0:c0 + n],
                            op0=ALU.mult, op1=ALU.add),
                            reads=pkb(b) + xk([m], c0, n), writes=xk([m], c0, n))
                    else:
                        hb = HB[hbi["i"]]
                        hbi["i"] = (hbi["i"] + 1) % len(HB)
                        tmp = TC[:, hb:hb + 2, :].rearrange("p a b -> p (a b)")[:, 0:n]
                        op("act", lambda h, b=b, tmp=tmp: h.activation(out=tmp, in_=psb(b, n), func=AF.Copy, scale=0.5),
                           reads=pkb(b), writes=tk_(hb, hb + 1))
                        op("pool", lambda h, m=m, tmp=tmp: h.tensor_tensor(out=X[:, m, c0:c0 + n], in0=X[:, m, c0:c0 + n],
                                                                          in1=tmp, op=ALU.add),
                           reads=tk_(hb, hb + 1) + xk([m], c0, n), writes=xk([m], c0, n))
                if t == len(FFN_TILES) - 1:
                    unit_done(un)
                if i == NPIECE - 1 and post is not None:
                    postq.append(post.gen(c0, n))
                    advance()

            for idx, (i, t) in enumerate(seq):
                if t == 0:
                    ucur[i] = unit(l, kind, i)
                base, rkeys, un = ucur[i]
                c0, n = FFN_TILES[t]
                ab = idx % 2
                for j in range(2):
                    bg, bu = 2 * j, 2 * j + 1
                    for (gu, b) in ((0, bg), (1, bu)):
                        group([lambda h, k=k, gu=gu, b=b, j=j: h.matmul(
                            psb(b, n),
                            lhsT=RING[:, base + k * 512 + gu * 256 + j * 128: base + k * 512 + gu * 256 + (j + 1) * 128],
                            rhs=HY[:, k, c0:c0 + n], start=(k == 0), stop=(k == 7)) for k in range(8)],
                            reads=rkeys + hk(range(8), c0, n), writes=pkb(b))
                    sg = TC[:, 4 * j:4 * j + 2, :].rearrange("p a b -> p (a b)")[:, 0:n]
                    sgk = tk_(4 * j, 4 * j + 1)
                    op("act", lambda h, sg=sg, bg=bg: h.activation(out=sg, in_=psb(bg, n), func=AF.Silu),
                       reads=pkb(bg), writes=sgk)
                    op("dve", lambda h, sg=sg, bu=bu, j=j: h.tensor_tensor(
                        out=ACTB[:, ab, j, 0:n], in0=sg, in1=psb(bu, n), op=ALU.mult),
                        reads=sgk + pkb(bu), writes=[("A", ab, j)])
                    advance()
                if prev is not None:
                    emit_D(*prev)
                prev = (i, t, ab, base, rkeys, un)
                pump(2)
                if mid is not None and i == 1 and t == 0:
                    mid()
            emit_D(*prev)
            while postq:
                advance()

        def transpose_in(region, nrow, ncol_chunks, src_dram, dst_fn, dst_keys):
            dma("sp", IO[0:nrow, region, 0:ncol_chunks * 128], src_dram, writes=[("IO", region)])
            for cc in range(ncol_chunks):
                hx = xps()
                group([lambda h: h.transpose(out=psh(hx, nrow), in_=IO[0:nrow, region, cc * 128:(cc + 1) * 128],
                                             identity=IDENT[0:nrow, 0:nrow])],
                      reads=[("IO", region), "IDENT"], writes=pk(hx))
                dst, src_view = dst_fn(cc, psh(hx, nrow))
                copy_any(dst, src_view, reads=pk(hx), writes=dst_keys)

        def transpose_out(region, src_ap_fn, src_keys, nrow, ncol_chunks, dst_dram, tmp_idx=None, tmp_view=None):
            for cc in range(ncol_chunks):
                src = src_ap_fn(cc)
                if tmp_idx is not None:
                    tmp = TC[:, tmp_idx, 0:nrow]
                    op("dve", lambda h: h.tensor_copy(out=tmp_view(tmp), in_=src), reads=src_keys, writes=tk_(tmp_idx))
                    lhs, rk = tmp, tk_(tmp_idx)
                else:
                    lhs, rk = src, src_keys
                hx = xps()
                group([lambda h: h.transpose(out=psh(hx, 128)[0:nrow, :], in_=lhs, identity=IDENT[:, :])],
                      reads=rk + ["IDENT"], writes=pk(hx))
                copy_any(IO[0:nrow, region, cc * 128:(cc + 1) * 128], psh(hx, 128)[0:nrow, :], reads=pk(hx),
                         writes=[("IO", region)])
            dma("act", dst_dram, IO[0:nrow, region, 0:ncol_chunks * 128], reads=[("IO", region)], writes=[],
                is_output=True)

        def state_prep(l):
            transpose_in(0, NS * 2, 2, sta[l, :, :],
                         lambda cc, ps: (XPA[:, cc, :, 0:2], ps.rearrange("p (b k) -> p b k", k=2)), ["XPA"])
            transpose_in(1, NS * 3, 4, stc[l, :, :],
                         lambda cc, ps: (XPC[:, cc, :, 0:3], ps.rearrange("p (b k) -> p b k", k=3)), ["XPC"])
            transpose_in(2, NS, 4, sth[l, :, :], lambda cc, ps: (HS0[:, cc, :], ps), ["HS0"])
            s = ld["s"]
            ld["s"] = (s + 1) % 3
            dma("sp", STG[0:120, s, 0:1024].rearrange("p (g c) -> p g c", g=4),
                stb[l, :, :].rearrange("(g p) c -> p g c", p=120), writes=[("S", s)])
            for g in range(4):
                for cc in range(2):
                    hx = xps()
                    group([lambda h: h.transpose(out=psh(hx, 120),
                                                 in_=STG[0:120, s, g * 256 + cc * 128: g * 256 + (cc + 1) * 128],
                                                 identity=IDENT[0:120, 0:120])],
                          reads=[("S", s), "IDENT"], writes=pk(hx))
                    copy_any(XPB[:, cc, 4 * g:4 * g + 4, 0:30], psh(hx, 120).rearrange("p (b k) -> p b k", k=30),
                             reads=pk(hx), writes=["XPB"])
            bdt = TC[:, 16:18, :].rearrange("p a b -> p (a b)")
            for w, src in enumerate((lwa, lwx)):
                dma("sp", bdt[:, w * 256:(w + 1) * 256].rearrange("p (c j) -> p c j", c=4),
                    src[l].rearrange("(c hh) i j -> (hh i) c j", hh=2), writes=tk_(16, 17))
            for w in range(2):
                for hh in range(2):
                    op("act", lambda h, w=w, hh=hh: h.activation(
                        out=BD[hh * 64:(hh + 1) * 64, w, :, hh * 64:(hh + 1) * 64],
                        in_=bdt[hh * 64:(hh + 1) * 64, w * 256:(w + 1) * 256].rearrange("p (c j) -> p c j", c=4),
                        func=AF.Copy), reads=tk_(16, 17), writes=["BD"])
            lam = prmc(l * PL + PC_LAM, 4)
            e_, u_, d_, ln_ = NSP[:, 8:12], NSP[:, 12:16], TC[:, 18, 0:4], TC[:, 18, 4:8]
            op("act", lambda h: h.activation(out=e_, in_=lam, func=AF.Exp, scale=-1.0), reads=["PRM"], writes=["NSPt"])
            op("dve", lambda h: h.tensor_scalar(out=u_, in0=e_, scalar1=1.0, scalar2=None, op0=ALU.add),
               reads=["NSPt"], writes=["NSPu"])
            op("dve", lambda h: h.tensor_scalar(out=d_, in0=u_, scalar1=-1.0, scalar2=1e-30, op0=ALU.add, op1=ALU.max),
               reads=["NSPu"], writes=tk_(18))
            op("dve", lambda h: h.reciprocal(out=d_, in_=d_), reads=tk_(18), writes=tk_(18))
            op("act", lambda h: h.activation(out=ln_, in_=u_, func=AF.Ln), reads=["NSPu"] + tk_(18), writes=tk_(18))
            op("dve", lambda h: h.tensor_tensor(out=ln_, in0=ln_, in1=e_, op=ALU.mult), reads=tk_(18) + ["NSPt"], writes=tk_(18))
            op("dve", lambda h: h.tensor_tensor(out=ln_, in0=ln_, in1=d_, op=ALU.mult), reads=tk_(18), writes=tk_(18))
            op("dve", lambda h: h.tensor_scalar(out=NSP[:, 0:4], in0=ln_, scalar1=-8.0, scalar2=None, op0=ALU.mult),
               reads=tk_(18), writes=["NSP"])
            op("dve", lambda h: h.tensor_scalar(out=NSP[:, 4:8], in0=ln_, scalar1=-16.0, scalar2=None, op0=ALU.mult),
               reads=tk_(18), writes=["NSP"])

        def win_mm(base, rkeys, ncols, col0, c0, n):
            hsl = mps()
            group([lambda h, k=k: h.matmul(psh(hsl, n), lhsT=RING[:, base + k * ncols + col0: base + k * ncols + col0 + 128],
                                           rhs=HY[:, k, c0:c0 + n], start=(k == 0), stop=(k == 7)) for k in range(8)],
                  reads=rkeys + hk(range(8), c0, n), writes=pk(hsl))
            return hsl

        def group_norm_to_yt(ysrc, ykeys, nch, n, gcol, yb):
            gn_stats(ysrc, ykeys, nch, n)
            gn_apply(ysrc, ykeys, nch, n, gcol, yb)

        def gn_apply(ysrc, ykeys, nch, n, gcol, yb):
            rs = TC[:, 19, 0:n]
            for cc in range(nch):
                op("dve", lambda h, cc=cc: h.scalar_tensor_tensor(out=YT[:, yb, cc, 0:n], in0=ysrc(cc), scalar=prmc(gcol + cc),
                                                                  in1=rs, op0=ALU.mult, op1=ALU.mult),
                   reads=ykeys(cc) + tk_(19) + ["PRM"], writes=[("YT", yb, cc)])

        def gn_stats(ysrc, ykeys, nch, n):
            for cc in range(nch):
                op("act", lambda h, cc=cc: h.activation(out=SQB[:, cc, 0:n], in_=ysrc(cc), func=AF.Square),
                   reads=ykeys(cc), writes=[("SQ", cc)])
            hs_ = sps()
            group([lambda h, cc=cc: h.matmul(psh(hs_, n), lhsT=ONES[:, :], rhs=SQB[:, cc, 0:n],
                                             start=(cc == 0), stop=(cc == nch - 1)) for cc in range(nch)],
                  reads=[("SQ", cc) for cc in range(nch)] + ["ONES"], writes=pk(hs_))
            rs = TC[:, 19, 0:n]
            op("act", lambda h: h.activation(out=rs, in_=psh(hs_, n), func=AF.Ln, scale=1.0 / (128 * nch), bias=EPS_RMS),
               reads=pk(hs_) + ["EPS"], writes=tk_(19))
            op("act", lambda h: h.activation(out=rs, in_=rs, func=AF.Exp, scale=-0.5), reads=tk_(19), writes=tk_(19))

        def wout_acc(base, rkeys, nch, yb, c0, n, offload=False):
            for m in range(8):
                if offload:
                    hsl = mps()
                    group([lambda h, kk=kk, m=m: h.matmul(psh(hsl, n),
                                                          lhsT=RING[:, base + kk * 1024 + m * 128: base + kk * 1024 + (m + 1) * 128],
                                                          rhs=YT[:, yb, kk, 0:n], start=(kk == 0), stop=(kk == nch - 1))
                           for kk in range(nch)],
                          reads=rkeys + [("YT", yb, kk) for kk in range(nch)], writes=pk(hsl))
                    ts_ = 28 + (m % 4)
                    tmp = TC[:, ts_, 0:n]
                    op("act", lambda h, hsl=hsl, tmp=tmp: h.activation(out=tmp, in_=psh(hsl, n), func=AF.Copy), reads=pk(hsl), writes=tk_(ts_))
                    op("pool", lambda h, m=m, tmp=tmp: h.tensor_tensor(out=X[:, m, c0:c0 + n], in0=X[:, m, c0:c0 + n], in1=tmp, op=ALU.add),
                       reads=tk_(ts_) + xk([m], c0, n), writes=xk([m], c0, n))
                    continue
                hsl = mps()
                group([lambda h, kk=kk, m=m: h.matmul(psh(hsl, n),
                                                      lhsT=RING[:, base + kk * 1024 + m * 128: base + kk * 1024 + (m + 1) * 128],
                                                      rhs=YT[:, yb, kk, 0:n], start=(kk == 0), stop=(kk == nch - 1))
                       for kk in range(nch)],
                      reads=rkeys + [("YT", yb, kk) for kk in range(nch)], writes=pk(hsl))
                op("dve", lambda h, m=m, hsl=hsl: h.tensor_tensor(out=X[:, m, c0:c0 + n], in0=psh(hsl, n), in1=X[:, m, c0:c0 + n],
                                                                  op=ALU.add),
                   reads=pk(hsl) + xk([m], c0, n), writes=xk([m], c0, n))

        def conv_prompt(buf, cc, K, n, wcol, bias_col, acc, acc_keys, buf_keys):
            if bias_col is None:
                op("dve", lambda h: h.tensor_scalar(out=acc, in0=buf[:, cc, 0:n], scalar1=prmc(wcol), scalar2=None, op0=ALU.mult),
                   reads=buf_keys + ["PRM"], writes=acc_keys)
            else:
                op("dve", lambda h: h.tensor_scalar(out=acc, in0=buf[:, cc, 0:n], scalar1=prmc(wcol), scalar2=prmc(bias_col),
                                                    op0=ALU.mult, op1=ALU.add),
                   reads=buf_keys + ["PRM"], writes=acc_keys)
            for k in range(1, K):
                op("dve", lambda h, k=k: h.scalar_tensor_tensor(out=acc, in0=buf[:, cc, k:k + n], scalar=prmc(wcol + k), in1=acc,
                                                                op0=ALU.mult, op1=ALU.add),
                   reads=buf_keys + acc_keys + ["PRM"], writes=acc_keys)

        def conv_sample(xpbuf, cc, K, wcol, bias_col, acc, acc_keys, buf_keys, prod_idx):
            prod = TC[:, prod_idx:prod_idx + 2, :].rearrange("p a b -> p (a b)")[:, 0:NS * K].rearrange("p (b k) -> p b k", k=K)
            pkeys = tk_(prod_idx, prod_idx + 1)
            wv = PRM[:, wcol:wcol + K].unsqueeze(1).to_broadcast([128, NS, K])
            op("dve", lambda h: h.tensor_tensor(out=prod, in0=xpbuf[:, cc, :, :], in1=wv, op=ALU.mult),
               reads=buf_keys + ["PRM"], writes=pkeys)
            op("dve", lambda h: h.tensor_reduce(out=acc, in_=prod, axis=AX.X, op=ALU.add), reads=pkeys, writes=acc_keys)
            if bias_col is not None:
                op("dve", lambda h: h.tensor_scalar(out=acc, in0=acc, scalar1=prmc(bias_col), scalar2=None, op0=ALU.add),
                   reads=acc_keys + ["PRM"], writes=acc_keys)

        def mixer(l, post):
            P0 = l * PL
            ntile = len(MIX_TILES)
            XCK = [("XC", c) for c in range(4)]
            PAv = XC[:, :, :].rearrange("p a b -> p (a b)")
            for cc in range(2):
                op("pool", lambda h, cc=cc: h.memset(PAv[:, cc * 514:cc * 514 + 2], 0.0), writes=XCK)
                op("pool", lambda h, cc=cc: h.memset(GL[:, cc, 0:30], 0.0), writes=[("GL", cc)])
            bA1, kA1, uA1 = unit(l, "MA1")
            bA2, kA2, uA2 = unit(l, "MA2")
            AT = [(0, 512), (512, 512), (1024, 512), (1536, 512), (2048, NS)]
            ntA = len(AT)
            SQBf = SQB[:, :, :].rearrange("p a b -> p (a b)")
            SQK = [("SQ", c) for c in range(8)]

            def Pw(s_, n):
                return TC[:, s_:s_ + 2, :].rearrange("p a b -> p (a b)")[:, 0:n]

            def ytv(yb, cc, n):
                return YT[:, yb, :, :].rearrange("p a b -> p (a b)")[:, cc * 512:cc * 512 + n]

            def ytk(yb, cc):
                return [("YT", yb, 2 * cc), ("YT", yb, 2 * cc + 1)]

            def ya_tmp(ti, cc, n):
                if ti == ntA - 1:
                    return TC[:, 8 + cc, 0:n], tk_(8 + cc)
                s_ = 8 + 4 * (ti % 2) + 2 * cc
                return Pw(s_, n), tk_(s_, s_ + 1)

            def a_back(ti):
                c0, n = AT[ti]
                samp = ti == ntA - 1
                yb = ti % 2
                yas = [ya_tmp(ti, cc, n) for cc in range(2)]
                for cc in range(2):
                    op("act", lambda h: h.activation(out=SQBf[:, cc * 512:cc * 512 + n], in_=yas[cc][0], func=AF.Square),
                       reads=yas[cc][1], writes=SQK)
                hs_ = sps()
                group([lambda h, cc=cc: h.matmul(psh(hs_, n), lhsT=ONES[:, :], rhs=SQBf[:, cc * 512:cc * 512 + n], start=(cc == 0), stop=(cc == 1))
                       for cc in range(2)], reads=SQK + ["ONES"], writes=pk(hs_))
                rs = Pw(16, n)
                rk = tk_(16, 17)
                op("act", lambda h: h.activation(out=rs, in_=psh(hs_, n), func=AF.Ln, scale=1.0 / 256, bias=EPS_RMS), reads=pk(hs_) + ["EPS"], writes=rk)
                op("act", lambda h: h.activation(out=rs, in_=rs, func=AF.Exp, scale=-0.5), reads=rk, writes=rk)
                yield
                for cc in range(2):
                    op("dve", lambda h: h.scalar_tensor_tensor(out=ytv(yb, cc, n), in0=yas[cc][0], scalar=prmc(P0 + PC_GG + cc),
                                                               in1=rs, op0=ALU.mult, op1=ALU.mult),
                       reads=yas[cc][1] + rk + ["PRM"], writes=ytk(yb, cc))
                for m in range(8):
                    hsl = mps()
                    group([lambda h, kk=kk, m=m: h.matmul(psh(hsl, n),
                                                          lhsT=RING[:, bA2 + kk * 1024 + m * 128: bA2 + kk * 1024 + (m + 1) * 128],
                                                          rhs=ytv(yb, kk, n), start=(kk == 0), stop=(kk == 1)) for kk in range(2)],
                          reads=kA2 + ytk(yb, 0) + ytk(yb, 1), writes=pk(hsl))
                    op("dve", lambda h, m=m, hsl=hsl: h.tensor_tensor(out=X[:, m, c0:c0 + n], in0=psh(hsl, n), in1=X[:, m, c0:c0 + n], op=ALU.add),
                       reads=pk(hsl) + xk([m], c0, n), writes=xk([m], c0, n))
                yield
                if ti == ntA - 2:
                    transpose_out(0, lambda cc: PAv[:, cc * 514 + n:cc * 514 + n + 2], XCK, 2, 2, opa[l, :, :])
                if samp:
                    transpose_out(0, lambda cc: XPA[:, cc, :, 1:3], ["XPA"], NS * 2, 2, osa[l, :, :], tmp_idx=18,
                                  tmp_view=lambda t: t.rearrange("p (b k) -> p b k", k=2))
                yield

            pend = None
            for ti, (c0, n) in enumerate(AT):
                samp = ti == ntA - 1
                for cc in range(2):
                    hC = win_mm(bA1, kA1, 768, 256 + cc * 128, c0, n)
                    hX = win_mm(bA1, kA1, 768, 512 + cc * 128, c0, n)
                    hB = win_mm(bA1, kA1, 768, cc * 128, c0, n)
                    csb, csk = Pw(2 * cc, n), tk_(2 * cc, 2 * cc + 1)
                    op("act", lambda h: h.activation(out=csb, in_=psh(hC, n), func=AF.Copy), reads=pk(hC), writes=csk)
                    acc, ack = Pw(4 + 2 * cc, n), tk_(4 + 2 * cc, 5 + 2 * cc)
                    pb = cc * 514
                    if not samp:
                        op("dve", lambda h: h.tensor_tensor(out=PAv[:, pb + 2:pb + 2 + n], in0=psh(hX, n), in1=csb, op=ALU.mult),
                           reads=pk(hX) + csk, writes=XCK)
                        for k in range(3):
                            wc = prmc(P0 + PC_CAW + cc * 3 + k)
                            if k == 0:
                                op("dve", lambda h: h.tensor_scalar(out=acc, in0=PAv[:, pb:pb + n], scalar1=wc, scalar2=None, op0=ALU.mult),
                                   reads=XCK + ["PRM"], writes=ack)
                            else:
                                op("dve", lambda h: h.scalar_tensor_tensor(out=acc, in0=PAv[:, pb + k:pb + k + n], scalar=wc, in1=acc,
                                                                           op0=ALU.mult, op1=ALU.add),
                                   reads=XCK + ack + ["PRM"], writes=ack)
                        if ti != ntA - 2:
                            op("dve", lambda h: h.tensor_copy(out=PAv[:, pb:pb + 2], in_=PAv[:, pb + n:pb + n + 2]),
                               reads=XCK, writes=XCK)
                    else:
                        op("dve", lambda h: h.tensor_tensor(out=XPA[:, cc, :, 2], in0=psh(hX, n), in1=csb, op=ALU.mult),
                           reads=pk(hX) + csk + ["XPA"], writes=["XPA"])
                        conv_sample(XPA, cc, 3, P0 + PC_CAW + cc * 3, None, acc, ack, ["XPA"], 24)
                    ya, yak = ya_tmp(ti, cc, n)
                    op("dve", lambda h: h.tensor_tensor(out=ya, in0=psh(hB, n), in1=acc, op=ALU.mult),
                       reads=pk(hB) + ack, writes=yak)
                    if pend is not None:
                        next(pend, None)
                if pend is not None:
                    for _ in pend:
                        pass
                pend = a_back(ti)
                pump(3)
            unit_done(uA1)
            carry = pend
            pump(2)
            bB1, kB1, uB1 = unit(l, "MB1")
            bB2, kB2, uB2 = unit(l, "MB2")
            BT = [(0, 512), (512, 512), (1024, 512), (1536, 512), (2048, NS)]
            ntB = len(BT)
            XCBf = XCB[:, :, :].rearrange("p a b -> p (a b)")
            SQBf = SQB[:, :, :].rearrange("p a b -> p (a b)")
            SQK = [("SQ", c) for c in range(8)]

            def Pw(s_, n):
                return TC[:, s_:s_ + 2, :].rearrange("p a b -> p (a b)")[:, 0:n]

            def btmp(ti, n):
                if ti == ntB - 1:
                    sl = [28, 29, 30, 31]
                    return [TC[:, x, 0:n] for x in sl], [tk_(x) for x in sl]
                base_ = 8 * (ti % 2)
                sl = [base_, base_ + 2, base_ + 4, base_ + 6]
                return [Pw(x, n) for x in sl], [tk_(x, x + 1) for x in sl]

            def ytv(yb, cc, n):
                return YT[:, yb, :, :].rearrange("p a b -> p (a b)")[:, cc * 512:cc * 512 + n]

            def ytk(yb, cc):
                return [("YT", yb, 2 * cc), ("YT", yb, 2 * cc + 1)]

            def b_back(ti):
                c0, n = BT[ti]
                samp = ti == ntB - 1
                yb = ti % 2
                (acc0, acc1, mean, var), (k0, k1, mk, vk) = btmp(ti, n)
                accs, acck = [acc0, acc1], [k0, k1]
                for cc in range(2):
                    op("act", lambda h: h.activation(out=XCBf[:, cc * 512:cc * 512 + n], in_=accs[cc], func=AF.Copy),
                       reads=acck[cc], writes=[("XCB", 2 * cc), ("XCB", 2 * cc + 1)])
                    op("act", lambda h: h.activation(out=SQBf[:, cc * 512:cc * 512 + n], in_=accs[cc], func=AF.Square),
                       reads=acck[cc], writes=SQK)
                hm, hq = sps(), sps()
                group([lambda h, cc=cc: h.matmul(psh(hm, n), lhsT=ONES[:, :], rhs=XCBf[:, cc * 512:cc * 512 + n], start=(cc == 0), stop=(cc == 1))
                       for cc in range(2)], reads=[("XCB", c) for c in range(4)] + ["ONES"], writes=pk(hm))
                group([lambda h, cc=cc: h.matmul(psh(hq, n), lhsT=ONES[:, :], rhs=SQBf[:, cc * 512:cc * 512 + n], start=(cc == 0), stop=(cc == 1))
                       for cc in range(2)], reads=SQK + ["ONES"], writes=pk(hq))
                op("act", lambda h: h.activation(out=mean, in_=psh(hm, n), func=AF.Copy, scale=1.0 / 256), reads=pk(hm), writes=mk)
                yield
                op("dve", lambda h: h.tensor_tensor(out=var, in0=mean, in1=mean, op=ALU.mult), reads=mk, writes=vk)
                op("dve", lambda h: h.scalar_tensor_tensor(out=var, in0=psh(hq, n), scalar=1.0 / 256, in1=var,
                                                           op0=ALU.mult, op1=ALU.subtract), reads=pk(hq) + vk, writes=vk)
                op("dve", lambda h: h.tensor_scalar(out=var, in0=var, scalar1=0.0, scalar2=None, op0=ALU.max), reads=vk, writes=vk)
                op("act", lambda h: h.activation(out=var, in_=var, func=AF.Ln, bias=EPS_LN), reads=vk + ["EPS"], writes=vk)
                op("act", lambda h: h.activation(out=var, in_=var, func=AF.Exp, scale=-0.5), reads=vk, writes=vk)
                yield
                for cc in range(2):
                    acc = accs[cc]
                    op("dve", lambda h: h.tensor_tensor(out=acc, in0=acc, in1=mean, op=ALU.subtract), reads=acck[cc] + mk, writes=acck[cc])
                    op("dve", lambda h: h.tensor_tensor(out=acc, in0=acc, in1=var, op=ALU.mult), reads=acck[cc] + vk, writes=acck[cc])
                    op("act", lambda h: h.activation(out=acc, in_=acc, func=AF.Silu, scale=prmc(P0 + PC_LNG + cc),
                                                     bias=prmc(P0 + PC_LNB + cc)), reads=acck[cc] + ["PRM"], writes=acck[cc])
                for cc in range(2):
                    op("act", lambda h: h.activation(out=SQBf[:, cc * 512:cc * 512 + n], in_=accs[cc], func=AF.Square),
                       reads=acck[cc], writes=SQK)
                hs_ = sps()
                group([lambda h, cc=cc: h.matmul(psh(hs_, n), lhsT=ONES[:, :], rhs=SQBf[:, cc * 512:cc * 512 + n], start=(cc == 0), stop=(cc == 1))
                       for cc in range(2)], reads=SQK + ["ONES"], writes=pk(hs_))
                rs = Pw(16, n)
                rk = tk_(16, 17)
                op("act", lambda h: h.activation(out=rs, in_=psh(hs_, n), func=AF.Ln, scale=1.0 / 256, bias=EPS_RMS), reads=pk(hs_) + ["EPS"], writes=rk)
                op("act", lambda h: h.activation(out=rs, in_=rs, func=AF.Exp, scale=-0.5), reads=rk, writes=rk)
                yield
                for cc in range(2):
                    op("dve", lambda h: h.scalar_tensor_tensor(out=ytv(yb, cc, n), in0=accs[cc], scalar=prmc(P0 + PC_GG + 2 + cc),
                                                               in1=rs, op0=ALU.mult, op1=ALU.mult),
                       reads=acck[cc] + rk + ["PRM"], writes=ytk(yb, cc))
                for m in range(8):
                    hsl = mps()
                    group([lambda h, kk=kk, m=m: h.matmul(psh(hsl, n),
                                                          lhsT=RING[:, bB2 + kk * 1024 + m * 128: bB2 + kk * 1024 + (m + 1) * 128],
                                                          rhs=ytv(yb, kk, n), start=(kk == 0), stop=(kk == 1)) for kk in range(2)],
                          reads=kB2 + ytk(yb, 0) + ytk(yb, 1), writes=pk(hsl))
                    op("dve", lambda h, m=m, hsl=hsl: h.tensor_tensor(out=X[:, m, c0:c0 + n], in0=psh(hsl, n), in1=X[:, m, c0:c0 + n], op=ALU.add),
                       reads=pk(hsl) + xk([m], c0, n), writes=xk([m], c0, n))
                if ti == ntB - 2:
                    transpose_out(1, lambda cc: GL[:, cc, n:n + 30], [("GL", 0), ("GL", 1)], 30, 2, opb[l, :, :])
                if samp:
                    s_ = ld["s"]
                    ld["s"] = (s_ + 1) % 3
                    for g in range(4):
                        for cc in range(2):
                            tmp = TC[:, 18, 0:120]
                            op("dve", lambda h: h.tensor_copy(out=tmp.rearrange("p (b k) -> p b k", k=30),
                                                              in_=XPB[:, cc, 4 * g:4 * g + 4, 1:31]), reads=["XPB"], writes=tk_(18))
                            hx = xps()
                            group([lambda h: h.transpose(out=psh(hx, 128)[0:120, :], in_=tmp, identity=IDENT[:, :])],
                                  reads=tk_(18) + ["IDENT"], writes=pk(hx))
                            copy_any(STG[0:120, s_, g * 256 + cc * 128: g * 256 + (cc + 1) * 128], psh(hx, 128)[0:120, :],
                                     reads=pk(hx), writes=[("S", s_)])
                    dma("act", osb[l, :, :].rearrange("(g p) c -> p g c", p=120),
                        STG[0:120, s_, 0:1024].rearrange("p (g c) -> p g c", g=4), reads=[("S", s_)], writes=[], is_output=True)
                yield

            pend = carry
            for ti, (c0, n) in enumerate(BT):
                samp = ti == ntB - 1
                (acc0, acc1, mean, var), (k0, k1, mk, vk) = btmp(ti, n)
                accs, acck = [acc0, acc1], [k0, k1]
                for cc in range(2):
                    hV = win_mm(bB1, kB1, 512, cc * 128, c0, n)
                    hG = win_mm(bB1, kB1, 512, 256 + cc * 128, c0, n)
                    sg = Pw(20 + 2 * cc, n)
                    sgk = tk_(20 + 2 * cc, 21 + 2 * cc)
                    op("act", lambda h: h.activation(out=sg, in_=psh(hG, n), func=AF.Sigmoid), reads=pk(hG), writes=sgk)
                    if not samp:
                        op("dve", lambda h: h.tensor_tensor(out=GL[:, cc, 30:30 + n], in0=psh(hV, n), in1=sg, op=ALU.mult),
                           reads=pk(hV) + sgk, writes=[("GL", cc)])
                    else:
                        op("dve", lambda h: h.tensor_tensor(out=XPB[:, cc, :, 30], in0=psh(hV, n), in1=sg, op=ALU.mult),
                           reads=pk(hV) + sgk + ["XPB"], writes=["XPB"])
                if not samp:
                    cnt = 0
                    for k in range(31):
                        for cc in range(2):
                            acc = accs[cc]
                            wcol = P0 + PC_CBW + cc * 31
                            if k == 0:
                                op("dve", lambda h: h.tensor_scalar(out=acc, in0=GL[:, cc, 0:n], scalar1=prmc(wcol),
                                                                    scalar2=prmc(P0 + PC_CBB + cc), op0=ALU.mult, op1=ALU.add),
                                   reads=[("GL", cc), "PRM"], writes=acck[cc])
                            else:
                                op("dve", lambda h: h.scalar_tensor_tensor(out=acc, in0=GL[:, cc, k:k + n], scalar=prmc(wcol + k),
                                                                           in1=acc, op0=ALU.mult, op1=ALU.add),
                                   reads=[("GL", cc), "PRM"] + acck[cc], writes=acck[cc])
                            cnt += 1
                            if pend is not None and cnt % 12 == 0:
                                next(pend, None)
                    if ti != ntB - 2:
                        for cc in range(2):
                            op("dve", lambda h: h.tensor_copy(out=GL[:, cc, 0:30], in_=GL[:, cc, n:n + 30]),
                               reads=[("GL", cc)], writes=[("GL", cc)])
                else:
                    for cc in range(2):
                        conv_sample(XPB, cc, 31, P0 + PC_CBW + cc * 31, P0 + PC_CBB + cc, accs[cc], acck[cc], ["XPB"], 24)
                if pend is not None:
                    for _ in pend:
                        pass
                if ti == 0:
                    unit_done(uA2)
                pend = b_back(ti)
                pump(3)
            unit_done(uB1)
            carry = pend
            pump(2)
            for cc in range(4):
                op("pool", lambda h, cc=cc: h.memset(XC[:, cc, 0:3], 0.0), writes=[("XC", cc)])
            bC1, kC1, uC1 = unit(l, "MC1")
            bC2, kC2, uC2 = unit(l, "MC2")
            CU = {}

            def c_tail(ti):
                c0, n = MIX_TILES[ti]
                yb = ti % 2
                Y0 = 24 + 4 * (ti % 2)
                gn_stats(lambda cc: TC[:, Y0 + cc, 0:n], lambda cc: tk_(Y0 + cc), 4, n)
                yield
                gn_apply(lambda cc: TC[:, Y0 + cc, 0:n], lambda cc: tk_(Y0 + cc), 4, n, P0 + PC_GG + 4, yb)
                wout_acc(CU["b"], CU["k"], 4, yb, c0, n)
                yield
                p2 = None
                if post is not None:
                    if getattr(post, "can_defer", False):
                        p2 = post(c0, n, defer=True)
                    else:
                        post(c0, n)
                yield
                if p2 is not None:
                    p2()
                yield

            pend = carry
            for ti, (c0, n) in enumerate(MIX_TILES):
                samp = ti == ntile - 1
                Y0 = 24 + 4 * (ti % 2)

                def step():
                    if pend is not None:
                        next(pend, None)
                for cc in range(4):
                    hX = win_mm(bC1, kC1, 512, cc * 128, c0, n)
                    if not samp:
                        copy_any(XC[:, cc, 3:3 + n], psh(hX, n), reads=pk(hX), writes=[("XC", cc)])
                    else:
                        copy_any(XPC[:, cc, :, 3], psh(hX, n), reads=pk(hX) + ["XPC"], writes=["XPC"])
                for cc in range(4):
                    hG = win_mm(bC2, kC2, 512, cc * 128, c0, n)
                    op("act", lambda h: h.activation(out=TC[:, 20 + cc, 0:n], in_=psh(hG, n), func=AF.Copy), reads=pk(hG), writes=tk_(20 + cc))
                    op("act", lambda h: h.activation(out=TC[:, Y0 + cc, 0:n], in_=psh(hG, n), func=AF.Square), reads=pk(hG), writes=tk_(Y0 + cc))
                if ti == ntile - 1:
                    unit_done(uC1)
                    unit_done(uC2)
                    pump(4)
                step()
                if not samp:
                    for k in range(4):
                        for cc in range(4):
                            xcv = TC[:, cc, 0:n]
                            wcol = P0 + PC_CCW + cc * 4
                            if k == 0:
                                op("dve", lambda h: h.tensor_scalar(out=xcv, in0=XC[:, cc, 0:n], scalar1=prmc(wcol),
                                                                    scalar2=prmc(P0 + PC_CCB + cc), op0=ALU.mult, op1=ALU.add),
                                   reads=[("XC", cc), "PRM"], writes=tk_(cc))
                            else:
                                op("dve", lambda h: h.scalar_tensor_tensor(out=xcv, in0=XC[:, cc, k:k + n], scalar=prmc(wcol + k),
                                                                           in1=xcv, op0=ALU.mult, op1=ALU.add),
                                   reads=[("XC", cc), "PRM"] + tk_(cc), writes=tk_(cc))
                    if ti != ntile - 2:
                        for cc in range(4):
                            op("dve", lambda h: h.tensor_copy(out=XC[:, cc, 0:3], in_=XC[:, cc, n:n + 3]),
                               reads=[("XC", cc)], writes=[("XC", cc)])
                else:
                    for cc in range(4):
                        conv_sample(XPC, cc, 4, P0 + PC_CCW + cc * 4, P0 + PC_CCB + cc, TC[:, cc, 0:n], tk_(cc), ["XPC"], 16)
                for cc in range(4):
                    op("act", lambda h: h.activation(out=XCB[:, cc, 0:n], in_=TC[:, cc, 0:n], func=AF.Copy), reads=tk_(cc), writes=[("XCB", cc)])
                for cc in range(4):
                    w = TC[:, Y0 + cc, 0:n]
                    op("pool", lambda h: h.tensor_scalar(out=w, in0=w, scalar1=0.044715, scalar2=1.0, op0=ALU.mult, op1=ALU.add),
                       reads=tk_(Y0 + cc), writes=tk_(Y0 + cc))
                    op("pool", lambda h: h.tensor_tensor(out=w, in0=w, in1=TC[:, 20 + cc, 0:n], op=ALU.mult), reads=tk_(Y0 + cc, 20 + cc), writes=tk_(Y0 + cc))
                for cc in range(4):
                    hr, hi = mps(), mps()
                    group([lambda h: h.matmul(psh(hr, n), lhsT=BD[:, 0, cc, :], rhs=XCB[:, cc, 0:n], start=True, stop=True)],
                          reads=["BD", ("XCB", cc)], writes=pk(hr))
                    group([lambda h: h.matmul(psh(hi, n), lhsT=BD[:, 1, cc, :], rhs=XCB[:, cc, 0:n], start=True, stop=True)],
                          reads=["BD", ("XCB", cc)], writes=pk(hi))
                    op("act", lambda h: h.activation(out=TC[:, 4 + cc, 0:n], in_=psh(hr, n), func=AF.Sigmoid,
                                                     bias=prmc(P0 + PC_BA + cc)), reads=pk(hr) + ["PRM"], writes=tk_(4 + cc))
                    op("act", lambda h: h.activation(out=TC[:, 8 + cc, 0:n], in_=psh(hi, n), func=AF.Sigmoid,
                                                     bias=prmc(P0 + PC_BX + cc)), reads=pk(hi) + ["PRM"], writes=tk_(8 + cc))
                for cc in range(4):
                    w = TC[:, Y0 + cc, 0:n]
                    op("act", lambda h: h.activation(out=w, in_=w, func=AF.Sigmoid, scale=1.5957691216057308), reads=tk_(Y0 + cc), writes=tk_(Y0 + cc))
                for cc in range(4):
                    op("act", lambda h: h.activation(out=TC[:, 12 + cc, 0:n], in_=TC[:, 4 + cc, 0:n], func=AF.Exp, scale=NSP[:, cc:cc + 1]),
                       reads=tk_(4 + cc) + ["NSP"], writes=tk_(12 + cc))
                    op("act", lambda h: h.activation(out=TC[:, 4 + cc, 0:n], in_=TC[:, 4 + cc, 0:n], func=AF.Exp, scale=NSP[:, 4 + cc:5 + cc]),
                       reads=tk_(4 + cc) + ["NSP"], writes=tk_(4 + cc))
                step()
                for cc in range(4):
                    w = TC[:, Y0 + cc, 0:n]
                    op("pool", lambda h: h.tensor_tensor(out=w, in0=w, in1=TC[:, 20 + cc, 0:n], op=ALU.mult), reads=tk_(Y0 + cc, 20 + cc), writes=tk_(Y0 + cc))
                for cc in range(4):
                    q = TC[:, 4 + cc, 0:n]
                    op("dve", lambda h: h.tensor_scalar(out=q, in0=q, scalar1=-1.0, scalar2=-1e-18, op0=ALU.add, op1=ALU.min),
                       reads=tk_(4 + cc), writes=tk_(4 + cc))
                    op("act", lambda h: h.activation(out=q, in_=q, func=AF.Ln, scale=-1.0), reads=tk_(4 + cc), writes=tk_(4 + cc))
                for cc in range(4):
                    bb = TC[:, 8 + cc, 0:n]
                    op("dve", lambda h: h.tensor_tensor(out=bb, in0=bb, in1=TC[:, cc, 0:n], op=ALU.mult), reads=tk_(8 + cc, cc), writes=tk_(8 + cc))
                for cc in range(4):
                    q = TC[:, 4 + cc, 0:n]
                    op("act", lambda h: h.activation(out=q, in_=q, func=AF.Exp, scale=0.5), reads=tk_(4 + cc), writes=tk_(4 + cc))
                step()
                for cc in range(4):
                    q = TC[:, 4 + cc, 0:n]
                    bb = TC[:, 8 + cc, 0:n]
                    op("dve", lambda h: h.tensor_tensor(out=bb, in0=bb, in1=q, op=ALU.mult), reads=tk_(8 + cc, 4 + cc), writes=tk_(8 + cc))
                    hs = TC[:, cc, 0:n]
                    a = TC[:, 12 + cc, 0:n]
                    if not samp:
                        init = 0.0 if ti == 0 else HC[:, cc:cc + 1]
                        op("dve", lambda h: h.tensor_tensor_scan(out=hs, data0=a, data1=bb, initial=init, op0=ALU.mult, op1=ALU.add),
                           reads=tk_(12 + cc, 8 + cc) + [("HC", cc)], writes=tk_(cc))
                        op("dve", lambda h: h.tensor_copy(out=HC[:, cc:cc + 1], in_=hs[:, n - 1:n]), reads=tk_(cc), writes=[("HC", cc)])
                    else:
                        op("dve", lambda h: h.tensor_tensor(out=hs, in0=a, in1=HS0[:, cc, :], op=ALU.mult), reads=tk_(12 + cc) + ["HS0"], writes=tk_(cc))
                        op("dve", lambda h: h.tensor_tensor(out=hs, in0=hs, in1=bb, op=ALU.add), reads=tk_(cc, 8 + cc), writes=tk_(cc))
                for cc in range(4):
                    w = TC[:, Y0 + cc, 0:n]
                    op("dve", lambda h: h.tensor_tensor(out=w, in0=w, in1=TC[:, cc, 0:n], op=ALU.mult), reads=tk_(Y0 + cc, cc), writes=tk_(Y0 + cc))
                step()
                if ti == ntile - 2:
                    transpose_out(1, lambda cc: XC[:, cc, n:n + 3], [("XC", c) for c in range(4)], 3, 4, opc[l, :, :])
                    transpose_out(2, lambda cc: HC[:, cc:cc + 1], [("HC", c) for c in range(4)], 1, 4, oph[l, :, :])
                if samp:
                    transpose_out(1, lambda cc: XPC[:, cc, :, 1:4], ["XPC"], NS * 3, 4, osc[l, :, :], tmp_idx=16,
                                  tmp_view=lambda t: t.rearrange("p (b k) -> p b k", k=3))
                    transpose_out(2, lambda cc: TC[:, cc, 0:NS], tk_(0, 1, 2, 3), NS, 4, osh[l, :, :])
                if pend is not None:
                    for _ in pend:
                        pass
                if ti == 0:
                    unit_done(uB2)
                    CU["b"], CU["k"], CU["u"] = unit(l, "MC3")
                pend = c_tail(ti)
                pump(2)
            for _ in pend:
                pass
            unit_done(uC1)
            unit_done(uC2)
            unit_done(CU["u"])
            pump(4)

        def out_rows(dst, r0, nrow, t0):
            s = ld["s"]
            ld["s"] = (s + 1) % 3
            for half in range(2):
                b = 6 + half
                fns = []
                for cc in range(4):
                    c = half * 4 + cc
                    fns.append(lambda h, c=c, cc=cc, b=b: h.transpose(out=PS[b][0:nrow, cc * 128:(cc + 1) * 128],
                                                                      in_=TC[:, 8 + c, t0:t0 + nrow], identity=IDENT[:, :]))
                group(fns, reads=tk_(*range(8 + half * 4, 8 + half * 4 + 4)) + ["IDENT"], writes=pkb(b))
                copy_any(STG[0:nrow, s, half * 512:(half + 1) * 512], PS[b][0:nrow, :], reads=pkb(b), writes=[("S", s)])
            dma("act", dst[r0:r0 + nrow, :], STG[0:nrow, s, 0:1024], reads=[("S", s)], writes=[], is_output=True)

        def final_post(c0, n):
            pe = min(c0 + n, T)
            for b0 in range(c0, pe, 256):
                nn = min(256, pe - b0)
                norm_block(b0, nn, DEPTH * PL, lambda c: TC[:, 8 + c, 0:nn], lambda c: tk_(8 + c))
                for r0 in range(b0, b0 + nn, 128):
                    out_rows(yp, r0, min(128, b0 + nn - r0), r0 - b0)
            if c0 + n > T:
                norm_block(T, NS, DEPTH * PL, lambda c: TC[:, 8 + c, 0:NS], lambda c: tk_(8 + c))
                out_rows(ys, 0, NS, 0)

        def final_gen(c0, n):
            yield
            yield
            yield
            final_post(c0, n)
            yield
        final_post.gen = final_gen

        def hy_post(gcol):
            def f(c0, n, defer=False):
                p2 = None
                bw = 272 if n % 256 == 16 else 256
                for b0 in range(c0, c0 + n, bw):
                    nn = min(bw, c0 + n - b0)
                    p2 = norm_block(b0, nn, gcol, lambda c, b0=b0, nn=nn: HY[:, c, b0:b0 + nn],
                                    lambda c, b0=b0, nn=nn: hk([c], b0, nn), defer=defer and n <= 256)
                return p2
            def gen(c0, n):
                bw = 272 if n % 256 == 16 else 256
                for b0 in range(c0, c0 + n, bw):
                    nn = min(bw, c0 + n - b0)
                    f_sq, f_mid, f_stt = norm_parts(b0, nn, gcol, lambda c, b0=b0, nn=nn: HY[:, c, b0:b0 + nn],
                                                    lambda c, b0=b0, nn=nn: hk([c], b0, nn))
                    f_sq()
                    yield
                    f_mid()
                    f_stt()
                    yield
            f.can_defer = True
            f.gen = gen
            return f

        PRE.append(hy_post(PC_NF1 if do_ffn else PC_NMIX))
        load_all_x()
        pump(6)
        for l in range(depth):
            last = l == depth - 1
            if do_mixer and not do_ffn:
                state_prep(l)
            if do_ffn:
                nxt = hy_post(l * PL + PC_NMIX) if do_mixer else hy_post(l * PL + PC_NF2)
                ffn(l, "F1", l * PL + PC_NF1, pre_norm=False, post=nxt,
                    mid=(lambda l=l: state_prep(l)) if do_mixer else None)
            if do_mixer:
                if do_ffn:
                    nxt = hy_post(l * PL + PC_NF2)
                else:
                    nxt = final_post if last else hy_post((l + 1) * PL + PC_NMIX)
                mixer(l, nxt)
            if do_ffn:
                nxt = final_post if last else hy_post((l + 1) * PL + PC_NF1)
                ffn(l, "F2", l * PL + PC_NF2, pre_norm=False, post=nxt)

        e = S.eng["act"]
        for tk in S.out_tickets:
            S._wait(e, tk)
    return nc


def _weight_stream(inputs):
    out = np.empty((DEPTH, 128, LSTREAM), np.float32)
    for l in range(DEPTH):
        parts = []
        for (kind, i, size, g0) in LUNITS:
            if kind in ("F1", "F2"):
                wu = inputs["w1_up" if kind == "F1" else "w2_up"][l]
                wd = inputs["w1_down" if kind == "F1" else "w2_down"][l]
                j0 = 2 * i
                up = wu.reshape(8, 128, 2, 22, 128)[:, :, :, j0:j0 + 2, :].transpose(1, 0, 2, 3, 4).reshape(128, -1)
                dn = wd.reshape(22, 128, 1024)[j0:j0 + 2].transpose(1, 0, 2).reshape(128, -1)
                parts += [up, dn]
            else:
                win = inputs["w_in"][l].reshape(8, 128, 2304)
                wout = inputs["w_out"][l].reshape(8, 128, 1024)
                if kind == "MA1":
                    parts.append(win[:, :, 0:768].transpose(1, 0, 2).reshape(128, -1))
                elif kind == "MA2":
                    parts.append(wout[0:2].transpose(1, 0, 2).reshape(128, -1))
                elif kind == "MB1":
                    parts.append(win[:, :, 768:1280].transpose(1, 0, 2).reshape(128, -1))
                elif kind == "MB2":
                    parts.append(wout[2:4].transpose(1, 0, 2).reshape(128, -1))
                elif kind == "MC1":
                    parts.append(win[:, :, 1792:2304].transpose(1, 0, 2).reshape(128, -1))
                elif kind == "MC2":
                    parts.append(win[:, :, 1280:1792].transpose(1, 0, 2).reshape(128, -1))
                elif kind == "MC3":
                    parts.append(wout[4:8].transpose(1, 0, 2).reshape(128, -1))
        out[l] = np.concatenate(parts, axis=1)
    return out


def _param_table(inputs):
    P = np.zeros((128, NPRM), np.float32)

    def fm(v, nch):
        return np.asarray(v, np.float32).reshape(nch, 128).T

    for l in range(DEPTH):
        o = l * PL
        P[:, o + PC_NF1:o + PC_NF1 + 8] = fm(inputs["norm_ffn1"][l], 8)
        P[:, o + PC_NMIX:o + PC_NMIX + 8] = fm(inputs["norm_mix"][l], 8)
        P[:, o + PC_NF2:o + PC_NF2 + 8] = fm(inputs["norm_ffn2"][l], 8)
        P[:, o + PC_GG:o + PC_GG + 8] = fm(inputs["grp_g"][l], 8)
        caw = inputs["conv_a_w"][l]
        P[:, o + PC_CAW:o + PC_CAW + 6] = caw.reshape(3, 2, 128).transpose(2, 1, 0).reshape(128, 6)
        cbw = inputs["conv_b_w"][l]
        P[:, o + PC_CBW:o + PC_CBW + 62] = cbw.reshape(31, 2, 128).transpose(2, 1, 0).reshape(128, 62)
        P[:, o + PC_CBB:o + PC_CBB + 2] = fm(inputs["conv_b_b"][l], 2)
        P[:, o + PC_LNG:o + PC_LNG + 2] = fm(inputs["ln_b_g"][l], 2)
        P[:, o + PC_LNB:o + PC_LNB + 2] = fm(inputs["ln_b_b"][l], 2)
        ccw = inputs["conv_c_w"][l]
        P[:, o + PC_CCW:o + PC_CCW + 16] = ccw.reshape(4, 4, 128).transpose(2, 1, 0).reshape(128, 16)
        P[:, o + PC_CCB:o + PC_CCB + 4] = fm(inputs["conv_c_b"][l], 4)
        P[:, o + PC_BA:o + PC_BA + 4] = fm(inputs["lru_ba"][l], 4)
        P[:, o + PC_BX:o + PC_BX + 4] = fm(inputs["lru_bx"][l], 4)
        P[:, o + PC_LAM:o + PC_LAM + 4] = fm(inputs["lru_lam"][l], 4)
    P[:, DEPTH * PL:DEPTH * PL + 8] = fm(inputs["final_norm"], 8)
    return P


def make_in_maps(inputs, cores):
    inputs = {k: np.asarray(v) for k, v in inputs.items()}
    wsr = _weight_stream(inputs)
    P = _param_table(inputs)
    lwa = np.ascontiguousarray(inputs["lru_wa"], np.float32)
    lwx = np.ascontiguousarray(inputs["lru_wx"], np.float32)
    maps = []
    for c in cores:
        sl = slice(NS * c, NS * (c + 1))
        maps.append({
            "xp": np.ascontiguousarray(inputs["x_prompt"][c], np.float32),
            "xs": np.ascontiguousarray(inputs["x_sample"][sl, 0, :], np.float32),
            "sta": np.ascontiguousarray(inputs["state_conv_a"][:, sl]).reshape(DEPTH, NS * 2, 256),
            "stb": np.ascontiguousarray(inputs["state_conv_b"][:, sl]).reshape(DEPTH, NS * 30, 256),
            "stc": np.ascontiguousarray(inputs["state_conv_c"][:, sl]).reshape(DEPTH, NS * 3, 512),
            "sth": np.ascontiguousarray(inputs["state_lru_h"][:, sl]).reshape(DEPTH, NS, 512),
            "ws": wsr, "prm": P, "lwa": lwa, "lwx": lwx,
        })
    return maps


def assemble(results, ncores):
    r = results
    y_prompt = np.stack([r[c]["yp"] for c in range(ncores)], 0)
    y_sample = np.concatenate([r[c]["ys"] for c in range(ncores)], 0)[:, None, :]
    p_a = np.stack([r[c]["opa"] for c in range(ncores)], 1)
    p_b = np.stack([r[c]["opb"] for c in range(ncores)], 1)
    p_c = np.stack([r[c]["opc"] for c in range(ncores)], 1)
    p_h = np.stack([r[c]["oph"][:, 0, :] for c in range(ncores)], 1)
    s_a = np.concatenate([r[c]["osa"].reshape(DEPTH, NS, 2, 256) for c in range(ncores)], 1)
    s_b = np.concatenate([r[c]["osb"].reshape(DEPTH, NS, 30, 256) for c in range(ncores)], 1)
    s_c = np.concatenate([r[c]["osc"].reshape(DEPTH, NS, 3, 512) for c in range(ncores)], 1)
    s_h = np.concatenate([r[c]["osh"] for c in range(ncores)], 1)
    return tuple(np.ascontiguousarray(a, dtype=np.float32) for a in
                 (y_prompt, y_sample, p_a, p_b, p_c, p_h, s_a, s_b, s_c, s_h))


def kernel(**inputs):
    nc = build()
    maps = make_in_maps(inputs, list(range(NCORES)))
    res = run_bass_kernel_spmd(nc, maps, core_ids=list(range(NCORES)))
    return assemble(res.results, NCORES)
```

```python
import numpy as np
from contextlib import ExitStack
import concourse.bass as bass
import concourse.mybir as mybir
from concourse.bass_utils import run_bass_kernel_spmd

F32 = mybir.dt.float32
BF16 = mybir.dt.bfloat16
AF = mybir.ActivationFunctionType
ALU = mybir.AluOpType
AX = mybir.AxisListType

NCORES = 8
D = 1024
T = 2048
NS = 16
NT = T + NS
DEPTH = 4
DFF = 2816
GR = 2048
NGR = 6
FFN_TILES = [(0, 512), (512, 512), (1024, 512), (1536, 256), (1792, 272)]
CH = 1024
MIX_TILES = [(256 * i, 256) for i in range(8)] + [(2048, 16)]
NPIECE = 11
RMS_EPS = 1e-6
LN_EPS = 1e-5

PC_NF1, PC_NMIX, PC_NF2, PC_GG = 0, 8, 16, 24
PC_CAW = 32
PC_CBW = 38
PC_CBB = 100
PC_LNG = 102
PC_LNB = 104
PC_CCW = 106
PC_CCB = 122
PC_BA = 126
PC_BX = 130
PC_LAM = 134
PL = 138
NPRM = PL * DEPTH + 8

UNIT_F = 6144


def layer_units():
    units = []
    for i in range(NPIECE):
        units.append(("F1", i, UNIT_F, 3 * (i % 2)))
    units.append(("MA1", 0, 6144, 3))
    units.append(("MA2", 0, 2048, 0))
    units.append(("MB1", 0, 4096, 1))
    units.append(("MB2", 0, 2048, 3))
    units.append(("MC1", 0, 4096, 4))
    units.append(("MC2", 0, 4096, 0))
    units.append(("MC3", 0, 4096, 2))
    for i in range(NPIECE):
        units.append(("F2", i, UNIT_F, 3 * ((i + 1) % 2)))
    return units


LUNITS = layer_units()
LSTREAM = sum(u[2] for u in LUNITS)


class Eng:
    def __init__(self, name, h, sem):
        self.name, self.h, self.sem, self.count, self.waited = name, h, sem, 0, {}


class Sched:
    def __init__(self, nc, es, ndma=24):
        self.nc = nc
        self.eng = {}
        for n, h in [("pe", nc.tensor), ("act", nc.scalar), ("dve", nc.vector),
                     ("pool", nc.gpsimd), ("sp", nc.sync)]:
            self.eng[n] = Eng(n, h, es.enter_context(nc.semaphore("sem_" + n)))
        self.dsem = [es.enter_context(nc.semaphore(f"dsem{i}")) for i in range(ndma)]
        self.dcount = [0] * ndma
        self.dlast = [None] * ndma
        self.dnext = 0
        self.lastw = {}
        self.readers = {}
        self.out_tickets = []

    def _wait(self, e, tk):
        if tk is None:
            return
        sem, val, name = tk
        if e.waited.get(name, 0) >= val:
            return
        e.h.wait_ge(sem, val)
        e.waited[name] = val

    def _deps(self, e, reads, writes):
        for k in reads:
            self._wait(e, self.lastw.get(k))
        for k in writes:
            self._wait(e, self.lastw.get(k))
            for tk in self.readers.get(k, {}).values():
                self._wait(e, tk)

    def _commit(self, tk, reads, writes):
        for k in reads:
            self.readers.setdefault(k, {})[tk[2]] = tk
        for k in writes:
            self.lastw[k] = tk
            self.readers[k] = {}

    def op(self, en, fn, reads=(), writes=()):
        e = self.eng[en]
        self._deps(e, reads, writes)
        ins = fn(e.h)
        e.count += 1
        ins.then_inc(e.sem, 1)
        tk = (e.sem, e.count, en)
        self._commit(tk, reads, writes)
        return tk

    def group(self, fns, reads=(), writes=()):
        e = self.eng["pe"]
        self._deps(e, reads, writes)
        for f in fns[:-1]:
            f(e.h)
        ins = fns[-1](e.h)
        e.count += 1
        ins.then_inc(e.sem, 1)
        tk = (e.sem, e.count, "pe")
        self._commit(tk, reads, writes)
        return tk

    def dma(self, qn, out, in_, reads=(), writes=(), is_output=False):
        e = self.eng[qn]
        self._deps(e, reads, writes)
        i = self.dnext
        self.dnext = (i + 1) % len(self.dsem)
        self._wait(e, self.dlast[i])
        ins = e.h.dma_start(out=out, in_=in_)
        self.dcount[i] += 16
        ins.then_inc(self.dsem[i], 16)
        tk = (self.dsem[i], self.dcount[i], f"d{i}")
        self.dlast[i] = tk
        self._commit(tk, reads, writes)
        if is_output:
            self.out_tickets.append(tk)
        return tk


def build(depth=DEPTH, do_mixer=True, do_ffn=True):
    nc = bass.Bass("TRN2", target_bir_lowering=False)

    def din(name, shape):
        return nc.dram_tensor(name, list(shape), F32, kind="ExternalInput").ap()

    def dout(name, shape):
        return nc.dram_tensor(name, list(shape), F32, kind="ExternalOutput").ap()

    xp = din("xp", [T, D])
    xs = din("xs", [NS, D])
    sta = din("sta", [DEPTH, NS * 2, 256])
    stb = din("stb", [DEPTH, NS * 30, 256])
    stc = din("stc", [DEPTH, NS * 3, 512])
    sth = din("sth", [DEPTH, NS, 512])
    ws = din("ws", [DEPTH, 128, LSTREAM])
    prm = din("prm", [128, NPRM])
    lwa = din("lwa", [DEPTH, 8, 64, 64])
    lwx = din("lwx", [DEPTH, 8, 64, 64])
    yp = dout("yp", [T, D])
    ys = dout("ys", [NS, D])
    opa = dout("opa", [DEPTH, 2, 256])
    opb = dout("opb", [DEPTH, 30, 256])
    opc = dout("opc", [DEPTH, 3, 512])
    oph = dout("oph", [DEPTH, 1, 512])
    osa = dout("osa", [DEPTH, NS * 2, 256])
    osb = dout("osb", [DEPTH, NS * 30, 256])
    osc = dout("osc", [DEPTH, NS * 3, 512])
    osh = dout("osh", [DEPTH, NS, 512])

    with ExitStack() as es:
        S = Sched(nc, es)
        op, group, dma = S.op, S.group, S.dma

        def sb(name, shape, dt):
            return es.enter_context(nc.sbuf_tensor(name, list(shape), dt))

        X = sb("X", [128, 8, NT], F32)
        HY = sb("HY", [128, 8, NT], BF16)
        RING = sb("RING", [128, NGR * GR], BF16)
        STG = sb("STG", [128, 3, CH], F32)
        ACTB = sb("ACTB", [128, 2, 2, 512], BF16)
        TC = sb("TC", [128, 32, 256], F32)
        SQB = sb("SQB", [128, 8, 272], BF16)
        XCB = sb("XCB", [128, 4, 256], BF16)
        YT = sb("YT", [128, 2, 4, 256], BF16)
        GL = sb("GL", [128, 2, 542], F32)
        XC = sb("XC", [128, 4, 259], F32)
        XPA = sb("XPA", [128, 2, NS, 3], F32)
        XPB = sb("XPB", [128, 2, NS, 31], F32)
        XPC = sb("XPC", [128, 4, NS, 4], F32)
        HS0 = sb("HS0", [128, 4, NS], F32)
        HC = sb("HC", [128, 4], F32)
        IO = sb("IO", [128, 3, 512], F32)
        PRM = sb("PRM", [128, NPRM], F32)
        ONES = sb("ONES", [128, 128], BF16)
        ONEF = sb("ONEF", [128, 128], F32)
        IDENT = sb("IDENT", [128, 128], F32)
        BD = sb("BD", [128, 2, 4, 128], BF16)
        NSP = sb("NSP", [128, 16], F32)
        PS = [es.enter_context(nc.psum_tensor(f"ps{i}", [128, 512], F32)) for i in range(8)]

        def xk(ms, c0, n):
            return [("X", m, b) for m in ms for b in range(c0 // 256, (c0 + n - 1) // 256 + 1)]

        def hk(ks, c0, n):
            return [("H", k, b) for k in ks for b in range(c0 // 256, (c0 + n - 1) // 256 + 1)]

        def tk_(*ids):
            return [("T", i) for i in ids]

        def psh(h, n=256):
            return PS[h][:, 0:n]

        def pk(h):
            return [("P", h)]

        def psb(b, n=512):
            return PS[b][:, 0:n]

        def pkb(b):
            return [("P", b)]

        rot = {"m": 0, "s": 0, "x": 0}

        def mps():
            h = rot["m"]
            rot["m"] = (h + 1) % 6
            return h

        def sps():
            h = 6 + rot["s"]
            rot["s"] = (rot["s"] + 1) % 2
            return h

        xps = sps

        def prmc(col, n=1):
            return PRM[:, col:col + n]

        alt = {"i": 0}

        def copy_any(out, in_, reads, writes):
            alt["i"] ^= 1
            if alt["i"]:
                return op("act", lambda h: h.activation(out=out, in_=in_, func=AF.Copy), reads, writes)
            return op("dve", lambda h: h.tensor_copy(out=out, in_=in_), reads, writes)

        units = []
        for l in range(depth):
            off = 0
            for (kind, i, size, g0) in LUNITS:
                units.append(dict(l=l, kind=kind, i=i, off=off, size=size, g0=g0, done=False))
                off += size
        uidx = {(u["l"], u["kind"], u["i"]): n for n, u in enumerate(units)}
        owner = [None] * NGR
        ld = {"u": 0, "c": 0, "s": 0}

        def pump(nmax=1):
            n = 0
            while n < nmax and ld["u"] < len(units):
                u = units[ld["u"]]
                skip = (u["kind"][0] == "F" and not do_ffn) or (u["kind"][0] == "M" and not do_mixer)
                if skip:
                    ld["u"] += 1
                    ld["c"] = 0
                    continue
                eo = ld["c"] * CH
                g = u["g0"] + eo // GR
                if eo % GR == 0:
                    if owner[g] is not None and not units[owner[g]]["done"]:
                        return
                    owner[g] = ld["u"]
                s = ld["s"]
                ld["s"] = (s + 1) % 3
                o = u["off"] + eo
                ro = u["g0"] * GR + eo
                dma("sp", STG[:, s, :], ws[u["l"], :, o:o + CH], reads=[], writes=[("S", s)])
                op("act", lambda h: h.activation(out=RING[:, ro:ro + CH], in_=STG[:, s, :], func=AF.Copy),
                   reads=[("S", s)], writes=[("R", g)])
                ld["c"] += 1
                if ld["c"] * CH >= u["size"]:
                    ld["u"] += 1
                    ld["c"] = 0
                n += 1

        def unit(l, kind, i=0):
            n = uidx[(l, kind, i)]
            u = units[n]
            guard = 0
            while ld["u"] <= n:
                before = (ld["u"], ld["c"])
                pump(1)
                guard += 1
                assert (ld["u"], ld["c"]) != before, f"loader stuck at unit {units[ld['u']]} wanting {u}"
            base = u["g0"] * GR
            keys = [("R", u["g0"] + j) for j in range(u["size"] // GR)]
            return base, keys, n

        def unit_done(n):
            units[n]["done"] = True

        dma("sp", PRM[:, :], prm[:, :], writes=["PRM"])
        op("pool", lambda h: h.memset(ONES[:], 1.0), writes=["ONES"])
        op("pool", lambda h: h.memset(ONEF[:], 1.0), writes=["ONEF"])
        op("pool", lambda h: h.memset(BD[:], 0.0), writes=["BD"])
        op("pool", lambda h: h.affine_select(out=IDENT[:], in_=ONEF[:], pattern=[[1, 128]],
                                             compare_op=ALU.is_equal, fill=0.0, base=0,
                                             channel_multiplier=-1),
           reads=["ONEF"], writes=["IDENT"])
        op("dve", lambda h: h.memset(HC[:], 0.0), writes=["HC"])

        def load_x_block(src, r0, nrow, c0):
            s = ld["s"]
            ld["s"] = (s + 1) % 3
            dma("sp", STG[0:nrow, s, 0:1024], src[r0:r0 + nrow, :], writes=[("S", s)])
            for half in range(2):
                b = 6 + half
                fns = []
                for cc in range(4):
                    c = half * 4 + cc
                    fns.append(lambda h, c=c, cc=cc, b=b: h.transpose(
                        out=PS[b][:, cc * 128:cc * 128 + nrow], in_=STG[0:nrow, s, c * 128:(c + 1) * 128],
                        identity=IDENT[0:nrow, 0:nrow]))
                group(fns, reads=[("S", s), "IDENT"], writes=pkb(b))
                src_ap = PS[b][:, :].rearrange("p (c t) -> p c t", c=4)[:, :, 0:nrow]
                copy_any(X[:, half * 4:half * 4 + 4, c0:c0 + nrow], src_ap, reads=pkb(b),
                         writes=xk(range(half * 4, half * 4 + 4), c0, nrow))

        PRE = []

        def load_x_tile(t):
            c0, n = FFN_TILES[t]
            pe_ = min(c0 + n, T)
            for r0 in range(c0, pe_, 128):
                load_x_block(xp, r0, 128, r0)
            if c0 + n > T:
                load_x_block(xs, 0, NS, T)
            PRE[0](c0, n)

        def norm_parts(c0, n, gcol, out_fn, out_keys_fn):
            rsk = tk_(19) if n <= 256 else tk_(19, 20)
            rs = TC[:, 19:21, :].rearrange("p a b -> p (a b)")[:, 0:n]
            st = {}

            def f_sq():
                op("act", lambda h: h.activation(out=SQB[:, :, 0:n], in_=X[:, :, c0:c0 + n], func=AF.Square),
                   reads=xk(range(8), c0, n), writes=[("SQ", c) for c in range(8)])

            def f_mid():
                hs_ = sps()
                group([lambda h, c=c: h.matmul(psh(hs_, n), lhsT=ONES[:, :], rhs=SQB[:, c, 0:n],
                                               start=(c == 0), stop=(c == 7)) for c in range(8)],
                      reads=[("SQ", c) for c in range(8)] + ["ONES"], writes=pk(hs_))
                op("act", lambda h: h.activation(out=rs, in_=psh(hs_, n), func=AF.Ln, scale=1.0 / D, bias=EPS_RMS[:, 0:1]),
                   reads=pk(hs_) + ["EPS"], writes=rsk)
                op("act", lambda h: h.activation(out=rs, in_=rs, func=AF.Exp, scale=-0.5),
                   reads=rsk, writes=rsk)

            def f_stt():
                for c in range(8):
                    op("dve", lambda h, c=c: h.scalar_tensor_tensor(out=out_fn(c), in0=X[:, c, c0:c0 + n],
                                                                    scalar=prmc(gcol + c), in1=rs,
                                                                    op0=ALU.mult, op1=ALU.mult),
                       reads=xk([c], c0, n) + rsk + ["PRM"], writes=out_keys_fn(c))
            return f_sq, f_mid, f_stt

        def norm_block(c0, n, gcol, out_fn, out_keys_fn, defer=False):
            f_sq, f_mid, f_stt = norm_parts(c0, n, gcol, out_fn, out_keys_fn)
            f_sq()
            f_mid()
            if defer:
                return f_stt
            f_stt()

        EPSB = sb("EPSB", [128, 4], F32)
        EPS_RMS = EPSB[:, 0:1]
        EPS_LN = EPSB[:, 1:2]
        op("dve", lambda h: h.memset(EPSB[:, 0:1], RMS_EPS), writes=["EPS0"])
        op("dve", lambda h: h.memset(EPSB[:, 1:2], LN_EPS), writes=["EPS1"])
        op("dve", lambda h: h.memset(EPSB[:, 2:3], 1.0), writes=["EPS2"])
        S.lastw["EPS"] = S.lastw["EPS1"]

        def norm_to_hy(gcol):
            for (c0, n) in MIX_TILES:
                norm_block(c0, n, gcol, lambda c, c0=c0, n=n: HY[:, c, c0:c0 + n],
                           lambda c, c0=c0, n=n: hk([c], c0, n))

        HB = [2, 6, 16]
        hbi = {"i": 0}
        def ffn(l, kind, gcol, pre_norm, post, mid=None, pre_tile=None):
            if pre_norm:
                norm_to_hy(gcol)
            seq = [(i, t) for i in range(NPIECE) for t in range(len(FFN_TILES))]
            prev = None
            ucur = {}
            postq = []

            def advance():
                while postq:
                    try:
                        next(postq[0])
                        return
                    except StopIteration:
                        postq.pop(0)

            def emit_D(i, t, ab, base, rkeys, un):
                c0, n = FFN_TILES[t]
                wd = base + 4096
                for m in range(8):
                    b = 4 + (m % 4)
                    group([lambda h, j=j, m=m, b=b: h.matmul(
                        psb(b, n), lhsT=RING[:, wd + j * 1024 + m * 128: wd + j * 1024 + (m + 1) * 128],
                        rhs=ACTB[:, ab, j, 0:n], start=(j == 0), stop=(j == 1)) for j in range(2)],
                        reads=rkeys + [("A", ab, 0), ("A", ab, 1)], writes=pkb(b))
                    if m % 2 == 0:
                        op("dve", lambda h, m=m, b=b: h.scalar_tensor_tensor(
                            out=X[:, m, c0:c0 + n], in0=psb(b, n), scalar=0.5, in1=X[:, m, c0:c0 + n],
                            op0=ALU.mult, op1=ALU.add),
                            reads=pkb(b) + xk([m], c0, n), writes=xk([m], c0, n))
                    else:
                        hb = HB[hbi["i"]]
                        hbi["i"] = (hbi["i"] + 1) % len(HB)
                        tmp = TC[:, hb:hb + 2, :].rearrange("p a b -> p (a b)")[:, 0:n]
                        op("act", lambda h, b=b, tmp=tmp: h.activation(out=tmp, in_=psb(b, n), func=AF.Copy, scale=0.5),
                           reads=pkb(b), writes=tk_(hb, hb + 1))
                        op("pool", lambda h, m=m, tmp=tmp: h.tensor_tensor(out=X[:, m, c0:c0 + n], in0=X[:, m, c0:c0 + n],
                                                                          in1=tmp, op=ALU.add),
                           reads=tk_(hb, hb + 1) + xk([m], c0, n), writes=xk([m], c0, n))
                if t == len(FFN_TILES) - 1:
                    unit_done(un)
                if i == NPIECE - 1 and post is not None:
                    postq.append(post.gen(c0, n))
                    advance()

            for idx, (i, t) in enumerate(seq):
                if t == 0:
                    ucur[i] = unit(l, kind, i)
                base, rkeys, un = ucur[i]
                c0, n = FFN_TILES[t]
                ab = idx % 2
                if pre_tile is not None and i == 0:
                    pre_tile(t)
                for j in range(2):
                    bg, bu = 2 * j, 2 * j + 1
                    for (gu, b) in ((0, bg), (1, bu)):
                        group([lambda h, k=k, gu=gu, b=b, j=j: h.matmul(
                            psb(b, n),
                            lhsT=RING[:, base + k * 512 + gu * 256 + j * 128: base + k * 512 + gu * 256 + (j + 1) * 128],
                            rhs=HY[:, k, c0:c0 + n], start=(k == 0), stop=(k == 7)) for k in range(8)],
                            reads=rkeys + hk(range(8), c0, n), writes=pkb(b))
                    sg = TC[:, 4 * j:4 * j + 2, :].rearrange("p a b -> p (a b)")[:, 0:n]
                    sgk = tk_(4 * j, 4 * j + 1)
                    op("act", lambda h, sg=sg, bg=bg: h.activation(out=sg, in_=psb(bg, n), func=AF.Silu),
                       reads=pkb(bg), writes=sgk)
                    op("dve", lambda h, sg=sg, bu=bu, j=j: h.tensor_tensor(
                        out=ACTB[:, ab, j, 0:n], in0=sg, in1=psb(bu, n), op=ALU.mult),
                        reads=sgk + pkb(bu), writes=[("A", ab, j)])
                    advance()
                if prev is not None:
                    emit_D(*prev)
                prev = (i, t, ab, base, rkeys, un)
                pump(2)
                if mid is not None and i == 1 and t == 0:
                    mid()
            emit_D(*prev)
            while postq:
                advance()

        def transpose_in(region, nrow, ncol_chunks, src_dram, dst_fn, dst_keys):
            dma("sp", IO[0:nrow, region, 0:ncol_chunks * 128], src_dram, writes=[("IO", region)])
            for cc in range(ncol_chunks):
                hx = xps()
                group([lambda h: h.transpose(out=psh(hx, nrow), in_=IO[0:nrow, region, cc * 128:(cc + 1) * 128],
                                             identity=IDENT[0:nrow, 0:nrow])],
                      reads=[("IO", region), "IDENT"], writes=pk(hx))
                dst, src_view = dst_fn(cc, psh(hx, nrow))
                copy_any(dst, src_view, reads=pk(hx), writes=dst_keys)

        def transpose_out(region, src_ap_fn, src_keys, nrow, ncol_chunks, dst_dram, tmp_idx=None, tmp_view=None):
            for cc in range(ncol_chunks):
                src = src_ap_fn(cc)
                if tmp_idx is not None:
                    tmp = TC[:, tmp_idx, 0:nrow]
                    op("dve", lambda h: h.tensor_copy(out=tmp_view(tmp), in_=src), reads=src_keys, writes=tk_(tmp_idx))
                    lhs, rk = tmp, tk_(tmp_idx)
                else:
                    lhs, rk = src, src_keys
                hx = xps()
                group([lambda h: h.transpose(out=psh(hx, 128)[0:nrow, :], in_=lhs, identity=IDENT[:, :])],
                      reads=rk + ["IDENT"], writes=pk(hx))
                copy_any(IO[0:nrow, region, cc * 128:(cc + 1) * 128], psh(hx, 128)[0:nrow, :], reads=pk(hx),
                         writes=[("IO", region)])
            dma("act", dst_dram, IO[0:nrow, region, 0:ncol_chunks * 128], reads=[("IO", region)], writes=[],
                is_output=True)

        def state_prep(l):
            transpose_in(0, NS * 2, 2, sta[l, :, :],
                         lambda cc, ps: (XPA[:, cc, :, 0:2], ps.rearrange("p (b k) -> p b k", k=2)), ["XPA"])
            transpose_in(1, NS * 3, 4, stc[l, :, :],
                         lambda cc, ps: (XPC[:, cc, :, 0:3], ps.rearrange("p (b k) -> p b k", k=3)), ["XPC"])
            transpose_in(2, NS, 4, sth[l, :, :], lambda cc, ps: (HS0[:, cc, :], ps), ["HS0"])
            s = ld["s"]
            ld["s"] = (s + 1) % 3
            dma("sp", STG[0:120, s, 0:1024].rearrange("p (g c) -> p g c", g=4),
                stb[l, :, :].rearrange("(g p) c -> p g c", p=120), writes=[("S", s)])
            for g in range(4):
                for cc in range(2):
                    hx = xps()
                    group([lambda h: h.transpose(out=psh(hx, 120),
                                                 in_=STG[0:120, s, g * 256 + cc * 128: g * 256 + (cc + 1) * 128],
                                                 identity=IDENT[0:120, 0:120])],
                          reads=[("S", s), "IDENT"], writes=pk(hx))
                    copy_any(XPB[:, cc, 4 * g:4 * g + 4, 0:30], psh(hx, 120).rearrange("p (b k) -> p b k", k=30),
                             reads=pk(hx), writes=["XPB"])
            bdt = TC[:, 16:18, :].rearrange("p a b -> p (a b)")
            for w, src in enumerate((lwa, lwx)):
                dma("sp", bdt[:, w * 256:(w + 1) * 256].rearrange("p (c j) -> p c j", c=4),
                    src[l].rearrange("(c hh) i j -> (hh i) c j", hh=2), writes=tk_(16, 17))
            for w in range(2):
                for hh in range(2):
                    op("act", lambda h, w=w, hh=hh: h.activation(
                        out=BD[hh * 64:(hh + 1) * 64, w, :, hh * 64:(hh + 1) * 64],
                        in_=bdt[hh * 64:(hh + 1) * 64, w * 256:(w + 1) * 256].rearrange("p (c j) -> p c j", c=4),
                        func=AF.Copy), reads=tk_(16, 17), writes=["BD"])
            lam = prmc(l * PL + PC_LAM, 4)
            e_, u_, d_, ln_ = NSP[:, 8:12], NSP[:, 12:16], TC[:, 18, 0:4], TC[:, 18, 4:8]
            op("act", lambda h: h.activation(out=e_, in_=lam, func=AF.Exp, scale=-1.0), reads=["PRM"], writes=["NSPt"])
            op("dve", lambda h: h.tensor_scalar(out=u_, in0=e_, scalar1=1.0, scalar2=None, op0=ALU.add),
               reads=["NSPt"], writes=["NSPu"])
            op("dve", lambda h: h.tensor_scalar(out=d_, in0=u_, scalar1=-1.0, scalar2=1e-30, op0=ALU.add, op1=ALU.max),
               reads=["NSPu"], writes=tk_(18))
            op("dve", lambda h: h.reciprocal(out=d_, in_=d_), reads=tk_(18), writes=tk_(18))
            op("act", lambda h: h.activation(out=ln_, in_=u_, func=AF.Ln), reads=["NSPu"] + tk_(18), writes=tk_(18))
            op("dve", lambda h: h.tensor_tensor(out=ln_, in0=ln_, in1=e_, op=ALU.mult), reads=tk_(18) + ["NSPt"], writes=tk_(18))
            op("dve", lambda h: h.tensor_tensor(out=ln_, in0=ln_, in1=d_, op=ALU.mult), reads=tk_(18), writes=tk_(18))
            op("dve", lambda h: h.tensor_scalar(out=NSP[:, 0:4], in0=ln_, scalar1=-8.0, scalar2=None, op0=ALU.mult),
               reads=tk_(18), writes=["NSP"])
            op("dve", lambda h: h.tensor_scalar(out=NSP[:, 4:8], in0=ln_, scalar1=-16.0, scalar2=None, op0=ALU.mult),
               reads=tk_(18), writes=["NSP"])

        def win_mm(base, rkeys, ncols, col0, c0, n):
            hsl = mps()
            group([lambda h, k=k: h.matmul(psh(hsl, n), lhsT=RING[:, base + k * ncols + col0: base + k * ncols + col0 + 128],
                                           rhs=HY[:, k, c0:c0 + n], start=(k == 0), stop=(k == 7)) for k in range(8)],
                  reads=rkeys + hk(range(8), c0, n), writes=pk(hsl))
            return hsl

        def group_norm_to_yt(ysrc, ykeys, nch, n, gcol, yb):
            gn_stats(ysrc, ykeys, nch, n)
            gn_apply(ysrc, ykeys, nch, n, gcol, yb)

        def gn_apply(ysrc, ykeys, nch, n, gcol, yb):
            rs = TC[:, 19, 0:n]
            for cc in range(nch):
                op("dve", lambda h, cc=cc: h.scalar_tensor_tensor(out=YT[:, yb, cc, 0:n], in0=ysrc(cc), scalar=prmc(gcol + cc),
                                                                  in1=rs, op0=ALU.mult, op1=ALU.mult),
                   reads=ykeys(cc) + tk_(19) + ["PRM"], writes=[("YT", yb, cc)])

        def gn_stats(ysrc, ykeys, nch, n):
            for cc in range(nch):
                op("act", lambda h, cc=cc: h.activation(out=SQB[:, cc, 0:n], in_=ysrc(cc), func=AF.Square),
                   reads=ykeys(cc), writes=[("SQ", cc)])
            hs_ = sps()
            group([lambda h, cc=cc: h.matmul(psh(hs_, n), lhsT=ONES[:, :], rhs=SQB[:, cc, 0:n],
                                             start=(cc == 0), stop=(cc == nch - 1)) for cc in range(nch)],
                  reads=[("SQ", cc) for cc in range(nch)] + ["ONES"], writes=pk(hs_))
            rs = TC[:, 19, 0:n]
            op("act", lambda h: h.activation(out=rs, in_=psh(hs_, n), func=AF.Ln, scale=1.0 / (128 * nch), bias=EPS_RMS),
               reads=pk(hs_) + ["EPS"], writes=tk_(19))
            op("act", lambda h: h.activation(out=rs, in_=rs, func=AF.Exp, scale=-0.5), reads=tk_(19), writes=tk_(19))

        def wout_acc(base, rkeys, nch, yb, c0, n, offload=False):
            for m in range(8):
                if offload:
                    hsl = mps()
                    group([lambda h, kk=kk, m=m: h.matmul(psh(hsl, n),
                                                          lhsT=RING[:, base + kk * 1024 + m * 128: base + kk * 1024 + (m + 1) * 128],
                                                          rhs=YT[:, yb, kk, 0:n], start=(kk == 0), stop=(kk == nch - 1))
                           for kk in range(nch)],
                          reads=rkeys + [("YT", yb, kk) for kk in range(nch)], writes=pk(hsl))
                    ts_ = 28 + (m % 4)
                    tmp = TC[:, ts_, 0:n]
                    op("act", lambda h, hsl=hsl, tmp=tmp: h.activation(out=tmp, in_=psh(hsl, n), func=AF.Copy), reads=pk(hsl), writes=tk_(ts_))
                    op("pool", lambda h, m=m, tmp=tmp: h.tensor_tensor(out=X[:, m, c0:c0 + n], in0=X[:, m, c0:c0 + n], in1=tmp, op=ALU.add),
                       reads=tk_(ts_) + xk([m], c0, n), writes=xk([m], c0, n))
                    continue
                hsl = mps()
                group([lambda h, kk=kk, m=m: h.matmul(psh(hsl, n),
                                                      lhsT=RING[:, base + kk * 1024 + m * 128: base + kk * 1024 + (m + 1) * 128],
                                                      rhs=YT[:, yb, kk, 0:n], start=(kk == 0), stop=(kk == nch - 1))
                       for kk in range(nch)],
                      reads=rkeys + [("YT", yb, kk) for kk in range(nch)], writes=pk(hsl))
                op("dve", lambda h, m=m, hsl=hsl: h.tensor_tensor(out=X[:, m, c0:c0 + n], in0=psh(hsl, n), in1=X[:, m, c0:c0 + n],
                                                                  op=ALU.add),
                   reads=pk(hsl) + xk([m], c0, n), writes=xk([m], c0, n))

        def conv_prompt(buf, cc, K, n, wcol, bias_col, acc, acc_keys, buf_keys):
            if bias_col is None:
                op("dve", lambda h: h.tensor_scalar(out=acc, in0=buf[:, cc, 0:n], scalar1=prmc(wcol), scalar2=None, op0=ALU.mult),
                   reads=buf_keys + ["PRM"], writes=acc_keys)
            else:
                op("dve", lambda h: h.tensor_scalar(out=acc, in0=buf[:, cc, 0:n], scalar1=prmc(wcol), scalar2=prmc(bias_col),
                                                    op0=ALU.mult, op1=ALU.add),
                   reads=buf_keys + ["PRM"], writes=acc_keys)
            for k in range(1, K):
                op("dve", lambda h, k=k: h.scalar_tensor_tensor(out=acc, in0=buf[:, cc, k:k + n], scalar=prmc(wcol + k), in1=acc,
                                                                op0=ALU.mult, op1=ALU.add),
                   reads=buf_keys + acc_keys + ["PRM"], writes=acc_keys)

        def conv_sample(xpbuf, cc, K, wcol, bias_col, acc, acc_keys, buf_keys, prod_idx):
            prod = TC[:, prod_idx:prod_idx + 2, :].rearrange("p a b -> p (a b)")[:, 0:NS * K].rearrange("p (b k) -> p b k", k=K)
            pkeys = tk_(prod_idx, prod_idx + 1)
            wv = PRM[:, wcol:wcol + K].unsqueeze(1).to_broadcast([128, NS, K])
            op("dve", lambda h: h.tensor_tensor(out=prod, in0=xpbuf[:, cc, :, :], in1=wv, op=ALU.mult),
               reads=buf_keys + ["PRM"], writes=pkeys)
            op("dve", lambda h: h.tensor_reduce(out=acc, in_=prod, axis=AX.X, op=ALU.add), reads=pkeys, writes=acc_keys)
            if bias_col is not None:
                op("dve", lambda h: h.tensor_scalar(out=acc, in0=acc, scalar1=prmc(bias_col), scalar2=None, op0=ALU.add),
                   reads=acc_keys + ["PRM"], writes=acc_keys)

        def mixer(l, post):
            P0 = l * PL
            ntile = len(MIX_TILES)
            XCK = [("XC", c) for c in range(4)]
            PAv = XC[:, :, :].rearrange("p a b -> p (a b)")
            for cc in range(2):
                op("pool", lambda h, cc=cc: h.memset(PAv[:, cc * 514:cc * 514 + 2], 0.0), writes=XCK)
                op("pool", lambda h, cc=cc: h.memset(GL[:, cc, 0:30], 0.0), writes=[("GL", cc)])
            bA1, kA1, uA1 = unit(l, "MA1")
            bA2, kA2, uA2 = unit(l, "MA2")
            AT = [(0, 512), (512, 512), (1024, 512), (1536, 512), (2048, NS)]
            ntA = len(AT)
            SQBf = SQB[:, :, :].rearrange("p a b -> p (a b)")
            SQK = [("SQ", c) for c in range(8)]

            def Pw(s_, n):
                return TC[:, s_:s_ + 2, :].rearrange("p a b -> p (a b)")[:, 0:n]

            def ytv(yb, cc, n):
                return YT[:, yb, :, :].rearrange("p a b -> p (a b)")[:, cc * 512:cc * 512 + n]

            def ytk(yb, cc):
                return [("YT", yb, 2 * cc), ("YT", yb, 2 * cc + 1)]

            def ya_tmp(ti, cc, n):
                if ti == ntA - 1:
                    return TC[:, 8 + cc, 0:n], tk_(8 + cc)
                s_ = 8 + 4 * (ti % 2) + 2 * cc
                return Pw(s_, n), tk_(s_, s_ + 1)

            def a_back(ti):
                c0, n = AT[ti]
                samp = ti == ntA - 1
                yb = ti % 2
                yas = [ya_tmp(ti, cc, n) for cc in range(2)]
                for cc in range(2):
                    op("act", lambda h: h.activation(out=SQBf[:, cc * 512:cc * 512 + n], in_=yas[cc][0], func=AF.Square),
                       reads=yas[cc][1], writes=SQK)
                hs_ = sps()
                group([lambda h, cc=cc: h.matmul(psh(hs_, n), lhsT=ONES[:, :], rhs=SQBf[:, cc * 512:cc * 512 + n], start=(cc == 0), stop=(cc == 1))
                       for cc in range(2)], reads=SQK + ["ONES"], writes=pk(hs_))
                rs = Pw(16, n)
                rk = tk_(16, 17)
                op("act", lambda h: h.activation(out=rs, in_=psh(hs_, n), func=AF.Ln, scale=1.0 / 256, bias=EPS_RMS), reads=pk(hs_) + ["EPS"], writes=rk)
                op("act", lambda h: h.activation(out=rs, in_=rs, func=AF.Exp, scale=-0.5), reads=rk, writes=rk)
                yield
                for cc in range(2):
                    op("dve", lambda h: h.scalar_tensor_tensor(out=ytv(yb, cc, n), in0=yas[cc][0], scalar=prmc(P0 + PC_GG + cc),
                                                               in1=rs, op0=ALU.mult, op1=ALU.mult),
                       reads=yas[cc][1] + rk + ["PRM"], writes=ytk(yb, cc))
                for m in range(8):
                    hsl = mps()
                    group([lambda h, kk=kk, m=m: h.matmul(psh(hsl, n),
                                                          lhsT=RING[:, bA2 + kk * 1024 + m * 128: bA2 + kk * 1024 + (m + 1) * 128],
                                                          rhs=ytv(yb, kk, n), start=(kk == 0), stop=(kk == 1)) for kk in range(2)],
                          reads=kA2 + ytk(yb, 0) + ytk(yb, 1), writes=pk(hsl))
                    op("dve", lambda h, m=m, hsl=hsl: h.tensor_tensor(out=X[:, m, c0:c0 + n], in0=psh(hsl, n), in1=X[:, m, c0:c0 + n], op=ALU.add),
                       reads=pk(hsl) + xk([m], c0, n), writes=xk([m], c0, n))
                yield
                if ti == ntA - 2:
                    transpose_out(0, lambda cc: PAv[:, cc * 514 + n:cc * 514 + n + 2], XCK, 2, 2, opa[l, :, :])
                if samp:
                    transpose_out(0, lambda cc: XPA[:, cc, :, 1:3], ["XPA"], NS * 2, 2, osa[l, :, :], tmp_idx=18,
                                  tmp_view=lambda t: t.rearrange("p (b k) -> p b k", k=2))
                yield

            pend = None
            for ti, (c0, n) in enumerate(AT):
                samp = ti == ntA - 1
                for cc in range(2):
                    hC = win_mm(bA1, kA1, 768, 256 + cc * 128, c0, n)
                    hX = win_mm(bA1, kA1, 768, 512 + cc * 128, c0, n)
                    hB = win_mm(bA1, kA1, 768, cc * 128, c0, n)
                    csb, csk = Pw(2 * cc, n), tk_(2 * cc, 2 * cc + 1)
                    op("act", lambda h: h.activation(out=csb, in_=psh(hC, n), func=AF.Copy), reads=pk(hC), writes=csk)
                    acc, ack = Pw(4 + 2 * cc, n), tk_(4 + 2 * cc, 5 + 2 * cc)
                    pb = cc * 514
                    if not samp:
                        op("dve", lambda h: h.tensor_tensor(out=PAv[:, pb + 2:pb + 2 + n], in0=psh(hX, n), in1=csb, op=ALU.mult),
                           reads=pk(hX) + csk, writes=XCK)
                        for k in range(3):
                            wc = prmc(P0 + PC_CAW + cc * 3 + k)
                            if k == 0:
                                op("dve", lambda h: h.tensor_scalar(out=acc, in0=PAv[:, pb:pb + n], scalar1=wc, scalar2=None, op0=ALU.mult),
                                   reads=XCK + ["PRM"], writes=ack)
                            else:
                                op("dve", lambda h: h.scalar_tensor_tensor(out=acc, in0=PAv[:, pb + k:pb + k + n], scalar=wc, in1=acc,
                                                                           op0=ALU.mult, op1=ALU.add),
                                   reads=XCK + ack + ["PRM"], writes=ack)
                        if ti != ntA - 2:
                            op("dve", lambda h: h.tensor_copy(out=PAv[:, pb:pb + 2], in_=PAv[:, pb + n:pb + n + 2]),
                               reads=XCK, writes=XCK)
                    else:
                        op("dve", lambda h: h.tensor_tensor(out=XPA[:, cc, :, 2], in0=psh(hX, n), in1=csb, op=ALU.mult),
                           reads=pk(hX) + csk + ["XPA"], writes=["XPA"])
                        conv_sample(XPA, cc, 3, P0 + PC_CAW + cc * 3, None, acc, ack, ["XPA"], 24)
                    ya, yak = ya_tmp(ti, cc, n)
                    op("dve", lambda h: h.tensor_tensor(out=ya, in0=psh(hB, n), in1=acc, op=ALU.mult),
                       reads=pk(hB) + ack, writes=yak)
                    if pend is not None:
                        next(pend, None)
                if pend is not None:
                    for _ in pend:
                        pass
                pend = a_back(ti)
                pump(3)
            unit_done(uA1)
            carry = pend
            pump(2)
            bB1, kB1, uB1 = unit(l, "MB1")
            bB2, kB2, uB2 = unit(l, "MB2")
            BT = [(0, 512), (512, 512), (1024, 512), (1536, 512), (2048, NS)]
            ntB = len(BT)
            XCBf = XCB[:, :, :].rearrange("p a b -> p (a b)")
            SQBf = SQB[:, :, :].rearrange("p a b -> p (a b)")
            SQK = [("SQ", c) for c in range(8)]

            def Pw(s_, n):
                return TC[:, s_:s_ + 2, :].rearrange("p a b -> p (a b)")[:, 0:n]

            def btmp(ti, n):
                if ti == ntB - 1:
                    sl = [28, 29, 30, 31]
                    return [TC[:, x, 0:n] for x in sl], [tk_(x) for x in sl]
                base_ = 8 * (ti % 2)
                sl = [base_, base_ + 2, base_ + 4, base_ + 6]
                return [Pw(x, n) for x in sl], [tk_(x, x + 1) for x in sl]

            def ytv(yb, cc, n):
                return YT[:, yb, :, :].rearrange("p a b -> p (a b)")[:, cc * 512:cc * 512 + n]

            def ytk(yb, cc):
                return [("YT", yb, 2 * cc), ("YT", yb, 2 * cc + 1)]

            def b_back(ti):
                c0, n = BT[ti]
                samp = ti == ntB - 1
                yb = ti % 2
                (acc0, acc1, mean, var), (k0, k1, mk, vk) = btmp(ti, n)
                accs, acck = [acc0, acc1], [k0, k1]
                for cc in range(2):
                    op("act", lambda h: h.activation(out=XCBf[:, cc * 512:cc * 512 + n], in_=accs[cc], func=AF.Copy),
                       reads=acck[cc], writes=[("XCB", 2 * cc), ("XCB", 2 * cc + 1)])
                    op("act", lambda h: h.activation(out=SQBf[:, cc * 512:cc * 512 + n], in_=accs[cc], func=AF.Square),
                       reads=acck[cc], writes=SQK)
                hm, hq = sps(), sps()
                group([lambda h, cc=cc: h.matmul(psh(hm, n), lhsT=ONES[:, :], rhs=XCBf[:, cc * 512:cc * 512 + n], start=(cc == 0), stop=(cc == 1))
                       for cc in range(2)], reads=[("XCB", c) for c in range(4)] + ["ONES"], writes=pk(hm))
                group([lambda h, cc=cc: h.matmul(psh(hq, n), lhsT=ONES[:, :], rhs=SQBf[:, cc * 512:cc * 512 + n], start=(cc == 0), stop=(cc == 1))
                       for cc in range(2)], reads=SQK + ["ONES"], writes=pk(hq))
                op("act", lambda h: h.activation(out=mean, in_=psh(hm, n), func=AF.Copy, scale=1.0 / 256), reads=pk(hm), writes=mk)
                yield
                op("dve", lambda h: h.tensor_tensor(out=var, in0=mean, in1=mean, op=ALU.mult), reads=mk, writes=vk)
                op("dve", lambda h: h.scalar_tensor_tensor(out=var, in0=psh(hq, n), scalar=1.0 / 256, in1=var,
                                                           op0=ALU.mult, op1=ALU.subtract), reads=pk(hq) + vk, writes=vk)
                op("dve", lambda h: h.tensor_scalar(out=var, in0=var, scalar1=0.0, scalar2=None, op0=ALU.max), reads=vk, writes=vk)
                op("act", lambda h: h.activation(out=var, in_=var, func=AF.Ln, bias=EPS_LN), reads=vk + ["EPS"], writes=vk)
                op("act", lambda h: h.activation(out=var, in_=var, func=AF.Exp, scale=-0.5), reads=vk, writes=vk)
                yield
                for cc in range(2):
                    acc = accs[cc]
                    op("dve", lambda h: h.tensor_tensor(out=acc, in0=acc, in1=mean, op=ALU.subtract), reads=acck[cc] + mk, writes=acck[cc])
                    op("dve", lambda h: h.tensor_tensor(out=acc, in0=acc, in1=var, op=ALU.mult), reads=acck[cc] + vk, writes=acck[cc])
                    op("act", lambda h: h.activation(out=acc, in_=acc, func=AF.Silu, scale=prmc(P0 + PC_LNG + cc),
                                                     bias=prmc(P0 + PC_LNB + cc)), reads=acck[cc] + ["PRM"], writes=acck[cc])
                for cc in range(2):
                    op("act", lambda h: h.activation(out=SQBf[:, cc * 512:cc * 512 + n], in_=accs[cc], func=AF.Square),
                       reads=acck[cc], writes=SQK)
                hs_ = sps()
                group([lambda h, cc=cc: h.matmul(psh(hs_, n), lhsT=ONES[:, :], rhs=SQBf[:, cc * 512:cc * 512 + n], start=(cc == 0), stop=(cc == 1))
                       for cc in range(2)], reads=SQK + ["ONES"], writes=pk(hs_))
                rs = Pw(16, n)
                rk = tk_(16, 17)
                op("act", lambda h: h.activation(out=rs, in_=psh(hs_, n), func=AF.Ln, scale=1.0 / 256, bias=EPS_RMS), reads=pk(hs_) + ["EPS"], writes=rk)
                op("act", lambda h: h.activation(out=rs, in_=rs, func=AF.Exp, scale=-0.5), reads=rk, writes=rk)
                yield
                for cc in range(2):
                    op("dve", lambda h: h.scalar_tensor_tensor(out=ytv(yb, cc, n), in0=accs[cc], scalar=prmc(P0 + PC_GG + 2 + cc),
                                                               in1=rs, op0=ALU.mult, op1=ALU.mult),
                       reads=acck[cc] + rk + ["PRM"], writes=ytk(yb, cc))
                for m in range(8):
                    hsl = mps()
                    group([lambda h, kk=kk, m=m: h.matmul(psh(hsl, n),
                                                          lhsT=RING[:, bB2 + kk * 1024 + m * 128: bB2 + kk * 1024 + (m + 1) * 128],
                                                          rhs=ytv(yb, kk, n), start=(kk == 0), stop=(kk == 1)) for kk in range(2)],
                          reads=kB2 + ytk(yb, 0) + ytk(yb, 1), writes=pk(hsl))
                    op("dve", lambda h, m=m, hsl=hsl: h.tensor_tensor(out=X[:, m, c0:c0 + n], in0=psh(hsl, n), in1=X[:, m, c0:c0 + n], op=ALU.add),
                       reads=pk(hsl) + xk([m], c0, n), writes=xk([m], c0, n))
                if ti == ntB - 2:
                    transpose_out(1, lambda cc: GL[:, cc, n:n + 30], [("GL", 0), ("GL", 1)], 30, 2, opb[l, :, :])
                if samp:
                    s_ = ld["s"]
                    ld["s"] = (s_ + 1) % 3
                    for g in range(4):
                        for cc in range(2):
                            tmp = TC[:, 18, 0:120]
                            op("dve", lambda h: h.tensor_copy(out=tmp.rearrange("p (b k) -> p b k", k=30),
                                                              in_=XPB[:, cc, 4 * g:4 * g + 4, 1:31]), reads=["XPB"], writes=tk_(18))
                            hx = xps()
                            group([lambda h: h.transpose(out=psh(hx, 128)[0:120, :], in_=tmp, identity=IDENT[:, :])],
                                  reads=tk_(18) + ["IDENT"], writes=pk(hx))
                            copy_any(STG[0:120, s_, g * 256 + cc * 128: g * 256 + (cc + 1) * 128], psh(hx, 128)[0:120, :],
                                     reads=pk(hx), writes=[("S", s_)])
                    dma("act", osb[l, :, :].rearrange("(g p) c -> p g c", p=120),
                        STG[0:120, s_, 0:1024].rearrange("p (g c) -> p g c", g=4), reads=[("S", s_)], writes=[], is_output=True)
                yield

            pend = carry
            for ti, (c0, n) in enumerate(BT):
                samp = ti == ntB - 1
                (acc0, acc1, mean, var), (k0, k1, mk, vk) = btmp(ti, n)
                accs, acck = [acc0, acc1], [k0, k1]
                for cc in range(2):
                    hV = win_mm(bB1, kB1, 512, cc * 128, c0, n)
                    hG = win_mm(bB1, kB1, 512, 256 + cc * 128, c0, n)
                    sg = Pw(20 + 2 * cc, n)
                    sgk = tk_(20 + 2 * cc, 21 + 2 * cc)
                    op("act", lambda h: h.activation(out=sg, in_=psh(hG, n), func=AF.Sigmoid), reads=pk(hG), writes=sgk)
                    if not samp:
                        op("dve", lambda h: h.tensor_tensor(out=GL[:, cc, 30:30 + n], in0=psh(hV, n), in1=sg, op=ALU.mult),
                           reads=pk(hV) + sgk, writes=[("GL", cc)])
                    else:
                        op("dve", lambda h: h.tensor_tensor(out=XPB[:, cc, :, 30], in0=psh(hV, n), in1=sg, op=ALU.mult),
                           reads=pk(hV) + sgk + ["XPB"], writes=["XPB"])
                if not samp:
                    cnt = 0
                    for k in range(31):
                        for cc in range(2):
                            acc = accs[cc]
                            wcol = P0 + PC_CBW + cc * 31
                            if k == 0:
                                op("dve", lambda h: h.tensor_scalar(out=acc, in0=GL[:, cc, 0:n], scalar1=prmc(wcol),
                                                                    scalar2=prmc(P0 + PC_CBB + cc), op0=ALU.mult, op1=ALU.add),
                                   reads=[("GL", cc), "PRM"], writes=acck[cc])
                            else:
                                op("dve", lambda h: h.scalar_tensor_tensor(out=acc, in0=GL[:, cc, k:k + n], scalar=prmc(wcol + k),
                                                                           in1=acc, op0=ALU.mult, op1=ALU.add),
                                   reads=[("GL", cc), "PRM"] + acck[cc], writes=acck[cc])
                            cnt += 1
                            if pend is not None and cnt % 12 == 0:
                                next(pend, None)
                    if ti != ntB - 2:
                        for cc in range(2):
                            op("dve", lambda h: h.tensor_copy(out=GL[:, cc, 0:30], in_=GL[:, cc, n:n + 30]),
                               reads=[("GL", cc)], writes=[("GL", cc)])
                else:
                    for cc in range(2):
                        conv_sample(XPB, cc, 31, P0 + PC_CBW + cc * 31, P0 + PC_CBB + cc, accs[cc], acck[cc], ["XPB"], 24)
                if pend is not None:
                    for _ in pend:
                        pass
                if ti == 0:
                    unit_done(uA2)
                pend = b_back(ti)
                pump(3)
            unit_done(uB1)
            carry = pend
            pump(2)
            for cc in range(4):
                op("pool", lambda h, cc=cc: h.memset(XC[:, cc, 0:3], 0.0), writes=[("XC", cc)])
            bC1, kC1, uC1 = unit(l, "MC1")
            bC2, kC2, uC2 = unit(l, "MC2")
            CU = {}

            def c_tail(ti):
                c0, n = MIX_TILES[ti]
                yb = ti % 2
                Y0 = 24 + 4 * (ti % 2)
                gn_stats(lambda cc: TC[:, Y0 + cc, 0:n], lambda cc: tk_(Y0 + cc), 4, n)
                yield
                gn_apply(lambda cc: TC[:, Y0 + cc, 0:n], lambda cc: tk_(Y0 + cc), 4, n, P0 + PC_GG + 4, yb)
                wout_acc(CU["b"], CU["k"], 4, yb, c0, n)
                yield
                p2 = None
                if post is not None:
                    if getattr(post, "can_defer", False):
                        p2 = post(c0, n, defer=True)
                    else:
                        post(c0, n)
                yield
                if p2 is not None:
                    p2()
                yield

            pend = carry
            for ti, (c0, n) in enumerate(MIX_TILES):
                samp = ti == ntile - 1
                Y0 = 24 + 4 * (ti % 2)

                def step():
                    if pend is not None:
                        next(pend, None)
                for cc in range(4):
                    hX = win_mm(bC1, kC1, 512, cc * 128, c0, n)
                    if not samp:
                        copy_any(XC[:, cc, 3:3 + n], psh(hX, n), reads=pk(hX), writes=[("XC", cc)])
                    else:
                        copy_any(XPC[:, cc, :, 3], psh(hX, n), reads=pk(hX) + ["XPC"], writes=["XPC"])
                for cc in range(4):
                    hG = win_mm(bC2, kC2, 512, cc * 128, c0, n)
                    op("act", lambda h: h.activation(out=TC[:, 20 + cc, 0:n], in_=psh(hG, n), func=AF.Copy), reads=pk(hG), writes=tk_(20 + cc))
                    op("act", lambda h: h.activation(out=TC[:, Y0 + cc, 0:n], in_=psh(hG, n), func=AF.Square), reads=pk(hG), writes=tk_(Y0 + cc))
                step()
                if not samp:
                    for k in range(4):
                        for cc in range(4):
                            xcv = TC[:, cc, 0:n]
                            wcol = P0 + PC_CCW + cc * 4
                            if k == 0:
                                op("dve", lambda h: h.tensor_scalar(out=xcv, in0=XC[:, cc, 0:n], scalar1=prmc(wcol),
                                                                    scalar2=prmc(P0 + PC_CCB + cc), op0=ALU.mult, op1=ALU.add),
                                   reads=[("XC", cc), "PRM"], writes=tk_(cc))
                            else:
                                op("dve", lambda h: h.scalar_tensor_tensor(out=xcv, in0=XC[:, cc, k:k + n], scalar=prmc(wcol + k),
                                                                           in1=xcv, op0=ALU.mult, op1=ALU.add),
                                   reads=[("XC", cc), "PRM"] + tk_(cc), writes=tk_(cc))
                    if ti != ntile - 2:
                        for cc in range(4):
                            op("dve", lambda h: h.tensor_copy(out=XC[:, cc, 0:3], in_=XC[:, cc, n:n + 3]),
                               reads=[("XC", cc)], writes=[("XC", cc)])
                else:
                    for cc in range(4):
                        conv_sample(XPC, cc, 4, P0 + PC_CCW + cc * 4, P0 + PC_CCB + cc, TC[:, cc, 0:n], tk_(cc), ["XPC"], 16)
                for cc in range(4):
                    op("act", lambda h: h.activation(out=XCB[:, cc, 0:n], in_=TC[:, cc, 0:n], func=AF.Copy), reads=tk_(cc), writes=[("XCB", cc)])
                for cc in range(4):
                    w = TC[:, Y0 + cc, 0:n]
                    op("pool", lambda h: h.tensor_scalar(out=w, in0=w, scalar1=0.044715, scalar2=1.0, op0=ALU.mult, op1=ALU.add),
                       reads=tk_(Y0 + cc), writes=tk_(Y0 + cc))
                    op("pool", lambda h: h.tensor_tensor(out=w, in0=w, in1=TC[:, 20 + cc, 0:n], op=ALU.mult), reads=tk_(Y0 + cc, 20 + cc), writes=tk_(Y0 + cc))
                for cc in range(4):
                    hr, hi = mps(), mps()
                    group([lambda h: h.matmul(psh(hr, n), lhsT=BD[:, 0, cc, :], rhs=XCB[:, cc, 0:n], start=True, stop=True)],
                          reads=["BD", ("XCB", cc)], writes=pk(hr))
                    group([lambda h: h.matmul(psh(hi, n), lhsT=BD[:, 1, cc, :], rhs=XCB[:, cc, 0:n], start=True, stop=True)],
                          reads=["BD", ("XCB", cc)], writes=pk(hi))
                    op("act", lambda h: h.activation(out=TC[:, 4 + cc, 0:n], in_=psh(hr, n), func=AF.Sigmoid,
                                                     bias=prmc(P0 + PC_BA + cc)), reads=pk(hr) + ["PRM"], writes=tk_(4 + cc))
                    op("act", lambda h: h.activation(out=TC[:, 8 + cc, 0:n], in_=psh(hi, n), func=AF.Sigmoid,
                                                     bias=prmc(P0 + PC_BX + cc)), reads=pk(hi) + ["PRM"], writes=tk_(8 + cc))
                for cc in range(4):
                    w = TC[:, Y0 + cc, 0:n]
                    op("act", lambda h: h.activation(out=w, in_=w, func=AF.Sigmoid, scale=1.5957691216057308), reads=tk_(Y0 + cc), writes=tk_(Y0 + cc))
                for cc in range(4):
                    op("act", lambda h: h.activation(out=TC[:, 12 + cc, 0:n], in_=TC[:, 4 + cc, 0:n], func=AF.Exp, scale=NSP[:, cc:cc + 1]),
                       reads=tk_(4 + cc) + ["NSP"], writes=tk_(12 + cc))
                    op("act", lambda h: h.activation(out=TC[:, 4 + cc, 0:n], in_=TC[:, 4 + cc, 0:n], func=AF.Exp, scale=NSP[:, 4 + cc:5 + cc]),
                       reads=tk_(4 + cc) + ["NSP"], writes=tk_(4 + cc))
                step()
                for cc in range(4):
                    w = TC[:, Y0 + cc, 0:n]
                    op("pool", lambda h: h.tensor_tensor(out=w, in0=w, in1=TC[:, 20 + cc, 0:n], op=ALU.mult), reads=tk_(Y0 + cc, 20 + cc), writes=tk_(Y0 + cc))
                for cc in range(4):
                    q = TC[:, 4 + cc, 0:n]
                    op("dve", lambda h: h.tensor_scalar(out=q, in0=q, scalar1=-1.0, scalar2=-1e-18, op0=ALU.add, op1=ALU.min),
                       reads=tk_(4 + cc), writes=tk_(4 + cc))
                    op("act", lambda h: h.activation(out=q, in_=q, func=AF.Ln, scale=-1.0), reads=tk_(4 + cc), writes=tk_(4 + cc))
                for cc in range(4):
                    bb = TC[:, 8 + cc, 0:n]
                    op("dve", lambda h: h.tensor_tensor(out=bb, in0=bb, in1=TC[:, cc, 0:n], op=ALU.mult), reads=tk_(8 + cc, cc), writes=tk_(8 + cc))
                for cc in range(4):
                    q = TC[:, 4 + cc, 0:n]
                    op("act", lambda h: h.activation(out=q, in_=q, func=AF.Exp, scale=0.5), reads=tk_(4 + cc), writes=tk_(4 + cc))
                step()
                for cc in range(4):
                    q = TC[:, 4 + cc, 0:n]
                    bb = TC[:, 8 + cc, 0:n]
                    op("dve", lambda h: h.tensor_tensor(out=bb, in0=bb, in1=q, op=ALU.mult), reads=tk_(8 + cc, 4 + cc), writes=tk_(8 + cc))
                    hs = TC[:, cc, 0:n]
                    a = TC[:, 12 + cc, 0:n]
                    if not samp:
                        init = 0.0 if ti == 0 else HC[:, cc:cc + 1]
                        op("dve", lambda h: h.tensor_tensor_scan(out=hs, data0=a, data1=bb, initial=init, op0=ALU.mult, op1=ALU.add),
                           reads=tk_(12 + cc, 8 + cc) + [("HC", cc)], writes=tk_(cc))
                        op("dve", lambda h: h.tensor_copy(out=HC[:, cc:cc + 1], in_=hs[:, n - 1:n]), reads=tk_(cc), writes=[("HC", cc)])
                    else:
                        op("dve", lambda h: h.tensor_tensor(out=hs, in0=a, in1=HS0[:, cc, :], op=ALU.mult), reads=tk_(12 + cc) + ["HS0"], writes=tk_(cc))
                        op("dve", lambda h: h.tensor_tensor(out=hs, in0=hs, in1=bb, op=ALU.add), reads=tk_(cc, 8 + cc), writes=tk_(cc))
                for cc in range(4):
                    w = TC[:, Y0 + cc, 0:n]
                    op("dve", lambda h: h.tensor_tensor(out=w, in0=w, in1=TC[:, cc, 0:n], op=ALU.mult), reads=tk_(Y0 + cc, cc), writes=tk_(Y0 + cc))
                step()
                if ti == ntile - 2:
                    transpose_out(1, lambda cc: XC[:, cc, n:n + 3], [("XC", c) for c in range(4)], 3, 4, opc[l, :, :])
                    transpose_out(2, lambda cc: HC[:, cc:cc + 1], [("HC", c) for c in range(4)], 1, 4, oph[l, :, :])
                if samp:
                    transpose_out(1, lambda cc: XPC[:, cc, :, 1:4], ["XPC"], NS * 3, 4, osc[l, :, :], tmp_idx=16,
                                  tmp_view=lambda t: t.rearrange("p (b k) -> p b k", k=3))
                    transpose_out(2, lambda cc: TC[:, cc, 0:NS], tk_(0, 1, 2, 3), NS, 4, osh[l, :, :])
                if pend is not None:
                    for _ in pend:
                        pass
                if ti == 0:
                    unit_done(uB2)
                    CU["b"], CU["k"], CU["u"] = unit(l, "MC3")
                pend = c_tail(ti)
                pump(2)
            for _ in pend:
                pass
            unit_done(uC1)
            unit_done(uC2)
            unit_done(CU["u"])
            pump(4)

        def out_rows(dst, r0, nrow, t0):
            s = ld["s"]
            ld["s"] = (s + 1) % 3
            for half in range(2):
                b = 6 + half
                fns = []
                for cc in range(4):
                    c = half * 4 + cc
                    fns.append(lambda h, c=c, cc=cc, b=b: h.transpose(out=PS[b][0:nrow, cc * 128:(cc + 1) * 128],
                                                                      in_=TC[:, 8 + c, t0:t0 + nrow], identity=IDENT[:, :]))
                group(fns, reads=tk_(*range(8 + half * 4, 8 + half * 4 + 4)) + ["IDENT"], writes=pkb(b))
                copy_any(STG[0:nrow, s, half * 512:(half + 1) * 512], PS[b][0:nrow, :], reads=pkb(b), writes=[("S", s)])
            dma("act", dst[r0:r0 + nrow, :], STG[0:nrow, s, 0:1024], reads=[("S", s)], writes=[], is_output=True)

        def final_post(c0, n):
            pe = min(c0 + n, T)
            for b0 in range(c0, pe, 256):
                nn = min(256, pe - b0)
                norm_block(b0, nn, DEPTH * PL, lambda c: TC[:, 8 + c, 0:nn], lambda c: tk_(8 + c))
                for r0 in range(b0, b0 + nn, 128):
                    out_rows(yp, r0, min(128, b0 + nn - r0), r0 - b0)
            if c0 + n > T:
                norm_block(T, NS, DEPTH * PL, lambda c: TC[:, 8 + c, 0:NS], lambda c: tk_(8 + c))
                out_rows(ys, 0, NS, 0)

        def final_gen(c0, n):
            yield
            yield
            yield
            final_post(c0, n)
            yield
        final_post.gen = final_gen

        def hy_post(gcol):
            def f(c0, n, defer=False):
                p2 = None
                bw = 272 if n % 256 == 16 else 256
                for b0 in range(c0, c0 + n, bw):
                    nn = min(bw, c0 + n - b0)
                    p2 = norm_block(b0, nn, gcol, lambda c, b0=b0, nn=nn: HY[:, c, b0:b0 + nn],
                                    lambda c, b0=b0, nn=nn: hk([c], b0, nn), defer=defer and n <= 256)
                return p2
            def gen(c0, n):
                bw = 272 if n % 256 == 16 else 256
                for b0 in range(c0, c0 + n, bw):
                    nn = min(bw, c0 + n - b0)
                    f_sq, f_mid, f_stt = norm_parts(b0, nn, gcol, lambda c, b0=b0, nn=nn: HY[:, c, b0:b0 + nn],
                                                    lambda c, b0=b0, nn=nn: hk([c], b0, nn))
                    f_sq()
                    yield
                    f_mid()
                    f_stt()
                    yield
            f.can_defer = True
            f.gen = gen
            return f

        PRE.append(hy_post(PC_NF1 if do_ffn else PC_NMIX))
        load_x_tile(0)
        pump(6)
        for l in range(depth):
            last = l == depth - 1
            if do_mixer and not do_ffn:
                state_prep(l)
            if do_ffn:
                nxt = hy_post(l * PL + PC_NMIX) if do_mixer else hy_post(l * PL + PC_NF2)
                ffn(l, "F1", l * PL + PC_NF1, pre_norm=False, post=nxt,
                    mid=(lambda l=l: state_prep(l)) if do_mixer else None,
                    pre_tile=(lambda t: load_x_tile(t + 1) if t + 1 < len(FFN_TILES) else None) if l == 0 else None)
            if do_mixer:
                if do_ffn:
                    nxt = hy_post(l * PL + PC_NF2)
                else:
                    nxt = final_post if last else hy_post((l + 1) * PL + PC_NMIX)
                mixer(l, nxt)
            if do_ffn:
                nxt = final_post if last else hy_post((l + 1) * PL + PC_NF1)
                ffn(l, "F2", l * PL + PC_NF2, pre_norm=False, post=nxt)

        e = S.eng["act"]
        for tk in S.out_tickets:
            S._wait(e, tk)
    return nc


def _weight_stream(inputs):
    out = np.empty((DEPTH, 128, LSTREAM), np.float32)
    for l in range(DEPTH):
        parts = []
        for (kind, i, size, g0) in LUNITS:
            if kind in ("F1", "F2"):
                wu = inputs["w1_up" if kind == "F1" else "w2_up"][l]
                wd = inputs["w1_down" if kind == "F1" else "w2_down"][l]
                j0 = 2 * i
                up = wu.reshape(8, 128, 2, 22, 128)[:, :, :, j0:j0 + 2, :].transpose(1, 0, 2, 3, 4).reshape(128, -1)
                dn = wd.reshape(22, 128, 1024)[j0:j0 + 2].transpose(1, 0, 2).reshape(128, -1)
                parts += [up, dn]
            else:
                win = inputs["w_in"][l].reshape(8, 128, 2304)
                wout = inputs["w_out"][l].reshape(8, 128, 1024)
                if kind == "MA1":
                    parts.append(win[:, :, 0:768].transpose(1, 0, 2).reshape(128, -1))
                elif kind == "MA2":
                    parts.append(wout[0:2].transpose(1, 0, 2).reshape(128, -1))
                elif kind == "MB1":
                    parts.append(win[:, :, 768:1280].transpose(1, 0, 2).reshape(128, -1))
                elif kind == "MB2":
                    parts.append(wout[2:4].transpose(1, 0, 2).reshape(128, -1))
                elif kind == "MC1":
                    parts.append(win[:, :, 1792:2304].transpose(1, 0, 2).reshape(128, -1))
                elif kind == "MC2":
                    parts.append(win[:, :, 1280:1792].transpose(1, 0, 2).reshape(128, -1))
                elif kind == "MC3":
                    parts.append(wout[4:8].transpose(1, 0, 2).reshape(128, -1))
        out[l] = np.concatenate(parts, axis=1)
    return out


def _param_table(inputs):
    P = np.zeros((128, NPRM), np.float32)

    def fm(v, nch):
        return np.asarray(v, np.float32).reshape(nch, 128).T

    for l in range(DEPTH):
        o = l * PL
        P[:, o + PC_NF1:o + PC_NF1 + 8] = fm(inputs["norm_ffn1"][l], 8)
        P[:, o + PC_NMIX:o + PC_NMIX + 8] = fm(inputs["norm_mix"][l], 8)
        P[:, o + PC_NF2:o + PC_NF2 + 8] = fm(inputs["norm_ffn2"][l], 8)
        P[:, o + PC_GG:o + PC_GG + 8] = fm(inputs["grp_g"][l], 8)
        caw = inputs["conv_a_w"][l]
        P[:, o + PC_CAW:o + PC_CAW + 6] = caw.reshape(3, 2, 128).transpose(2, 1, 0).reshape(128, 6)
        cbw = inputs["conv_b_w"][l]
        P[:, o + PC_CBW:o + PC_CBW + 62] = cbw.reshape(31, 2, 128).transpose(2, 1, 0).reshape(128, 62)
        P[:, o + PC_CBB:o + PC_CBB + 2] = fm(inputs["conv_b_b"][l], 2)
        P[:, o + PC_LNG:o + PC_LNG + 2] = fm(inputs["ln_b_g"][l], 2)
        P[:, o + PC_LNB:o + PC_LNB + 2] = fm(inputs["ln_b_b"][l], 2)
        ccw = inputs["conv_c_w"][l]
        P[:, o + PC_CCW:o + PC_CCW + 16] = ccw.reshape(4, 4, 128).transpose(2, 1, 0).reshape(128, 16)
        P[:, o + PC_CCB:o + PC_CCB + 4] = fm(inputs["conv_c_b"][l], 4)
        P[:, o + PC_BA:o + PC_BA + 4] = fm(inputs["lru_ba"][l], 4)
        P[:, o + PC_BX:o + PC_BX + 4] = fm(inputs["lru_bx"][l], 4)
        P[:, o + PC_LAM:o + PC_LAM + 4] = fm(inputs["lru_lam"][l], 4)
    P[:, DEPTH * PL:DEPTH * PL + 8] = fm(inputs["final_norm"], 8)
    return P


def make_in_maps(inputs, cores):
    inputs = {k: np.asarray(v) for k, v in inputs.items()}
    wsr = _weight_stream(inputs)
    P = _param_table(inputs)
    lwa = np.ascontiguousarray(inputs["lru_wa"], np.float32)
    lwx = np.ascontiguousarray(inputs["lru_wx"], np.float32)
    maps = []
    for c in cores:
        sl = slice(NS * c, NS * (c + 1))
        maps.append({
            "xp": np.ascontiguousarray(inputs["x_prompt"][c], np.float32),
            "xs": np.ascontiguousarray(inputs["x_sample"][sl, 0, :], np.float32),
            "sta": np.ascontiguousarray(inputs["state_conv_a"][:, sl]).reshape(DEPTH, NS * 2, 256),
            "stb": np.ascontiguousarray(inputs["state_conv_b"][:, sl]).reshape(DEPTH, NS * 30, 256),
            "stc": np.ascontiguousarray(inputs["state_conv_c"][:, sl]).reshape(DEPTH, NS * 3, 512),
            "sth": np.ascontiguousarray(inputs["state_lru_h"][:, sl]).reshape(DEPTH, NS, 512),
            "ws": wsr, "prm": P, "lwa": lwa, "lwx": lwx,
        })
    return maps


def assemble(results, ncores):
    r = results
    y_prompt = np.stack([r[c]["yp"] for c in range(ncores)], 0)
    y_sample = np.concatenate([r[c]["ys"] for c in range(ncores)], 0)[:, None, :]
    p_a = np.stack([r[c]["opa"] for c in range(ncores)], 1)
    p_b = np.stack([r[c]["opb"] for c in range(ncores)], 1)
    p_c = np.stack([r[c]["opc"] for c in range(ncores)], 1)
    p_h = np.stack([r[c]["oph"][:, 0, :] for c in range(ncores)], 1)
    s_a = np.concatenate([r[c]["osa"].reshape(DEPTH, NS, 2, 256) for c in range(ncores)], 1)
    s_b = np.concatenate([r[c]["osb"].reshape(DEPTH, NS, 30, 256) for c in range(ncores)], 1)
    s_c = np.concatenate([r[c]["osc"].reshape(DEPTH, NS, 3, 512) for c in range(ncores)], 1)
    s_h = np.concatenate([r[c]["osh"] for c in range(ncores)], 1)
    return tuple(np.ascontiguousarray(a, dtype=np.float32) for a in
                 (y_prompt, y_sample, p_a, p_b, p_c, p_h, s_a, s_b, s_c, s_h))


def kernel(**inputs):
    nc = build()
    maps = make_in_maps(inputs, list(range(NCORES)))
    res = run_bass_kernel_spmd(nc, maps, core_ids=list(range(NCORES)))
    return assemble(res.results, NCORES)
```
